# Optimizing a Trainium2 kernel written in Bass

```python
import math
import jax, jax.numpy as jnp
from jax import lax
import numpy as np

D_MODEL = 1024
BATCH = 8
SEQ = 8192
DEPTH = 1
DEC_BATCH = 4
DEC_SEQ = 8192
PAST_LEN = 128

HEAD_DIM = 64
A_HEADS = 8
A_KV = 2
B_HEADS = 8
B_KV = 2
A_WIDTH = A_HEADS * HEAD_DIM
B_WIDTH = B_HEADS * HEAD_DIM
WINDOW = 128
BLOCK = 128
N_META = 16
GRID_W = 64
ROPE_THETA = 10000.0
LN_EPS = 1e-5
RMS_EPS = 1e-6
NEG_INF = -1e30
ALPHA = (2.0 * DEPTH) ** 0.25
BETA = (8.0 * DEPTH) ** -0.25
IN_SPLITS = (A_HEADS * HEAD_DIM, A_KV * HEAD_DIM, A_KV * HEAD_DIM, A_WIDTH,
             B_HEADS * HEAD_DIM, B_KV * HEAD_DIM, B_KV * HEAD_DIM, B_WIDTH,
             D_MODEL, D_MODEL)
D_IN = sum(IN_SPLITS)

kernel_name = "hybrid_window_axial_gqa_encoder"


def _rope_angles(pos, dim):
    inv = ROPE_THETA ** (-jnp.arange(0, dim, 2, dtype=jnp.float32) / dim)
    return pos.astype(jnp.float32)[:, None] * inv[None, :]


def _rotate(x, ang):
    d2 = x.shape[-1] // 2
    cos = jnp.cos(ang)[:, None, :].astype(x.dtype)
    sin = jnp.sin(ang)[:, None, :].astype(x.dtype)
    x1, x2 = x[..., :d2], x[..., d2:]
    return jnp.concatenate([x1 * cos - x2 * sin, x1 * sin + x2 * cos], axis=-1)


def _axial_rope(x, row, col):
    h = x.shape[-1] // 2
    return jnp.concatenate([_rotate(x[..., :h], _rope_angles(row, h)),
                            _rotate(x[..., h:], _rope_angles(col, h))], axis=-1)


def _rms_norm(x, g):
    xf = x.astype(jnp.float32)
    y = xf * lax.rsqrt(jnp.mean(xf * xf, axis=-1, keepdims=True) + RMS_EPS) * g.astype(jnp.float32)
    return y.astype(x.dtype)


def _layer_norm(x, g, b):
    xf = x.astype(jnp.float32)
    mu = jnp.mean(xf, axis=-1, keepdims=True)
    var = jnp.mean(jnp.square(xf - mu), axis=-1, keepdims=True)
    y = (xf - mu) * lax.rsqrt(var + LN_EPS) * g.astype(jnp.float32) + b.astype(jnp.float32)
    return y.astype(x.dtype)


def _window_attention(q, k, v, sink):
    Bn, L, H, hd = q.shape
    KV = k.shape[2]
    G = H // KV
    S = L - N_META
    nb = S // BLOCK
    scale = hd ** -0.5
    qm, qr = q[:, :N_META], q[:, N_META:]
    km, kr = k[:, :N_META], k[:, N_META:]
    vm, vr = v[:, :N_META], v[:, N_META:]
    sink_g = sink.astype(jnp.float32).reshape(KV, G)

    qb = qr.reshape(Bn, nb, BLOCK, KV, G, hd)
    pad = ((0, 0), (BLOCK, BLOCK), (0, 0), (0, 0))
    kp = jnp.pad(kr, pad).reshape(Bn, nb + 2, BLOCK, KV, hd)
    vp = jnp.pad(vr, pad).reshape(Bn, nb + 2, BLOCK, KV, hd)
    kb = jnp.concatenate([kp[:, :-2], kp[:, 1:-1], kp[:, 2:]], axis=2)
    vb = jnp.concatenate([vp[:, :-2], vp[:, 1:-1], vp[:, 2:]], axis=2)
    s_band = jnp.einsum('bnqkgd,bnskd->bnkgqs', qb, kb).astype(jnp.float32) * scale
    blk = jnp.arange(nb)
    key_idx = blk[:, None] * BLOCK - BLOCK + jnp.arange(3 * BLOCK)[None, :]
    q_idx = blk[:, None] * BLOCK + jnp.arange(BLOCK)[None, :]
    rel = key_idx[:, None, :] - q_idx[:, :, None]
    valid = (jnp.abs(rel) <= WINDOW) & (key_idx[:, None, :] >= 0) & (key_idx[:, None, :] < S)
    s_band = jnp.where(valid[None, :, None, None], s_band, NEG_INF)
    s_meta = jnp.einsum('bnqkgd,bmkd->bnkgqm', qb, km).astype(jnp.float32) * scale
    s_sink = jnp.broadcast_to(sink_g[None, None, :, :, None, None], s_meta.shape[:-1] + (1,))
    p = jax.nn.softmax(jnp.concatenate([s_meta, s_band, s_sink], axis=-1), axis=-1).astype(v.dtype)
    o_real = (jnp.einsum('bnkgqm,bmkd->bnqkgd', p[..., :N_META], vm)
              + jnp.einsum('bnkgqs,bnskd->bnqkgd', p[..., N_META:N_META + 3 * BLOCK], vb))
    o_real = o_real.reshape(Bn, S, H, hd)

    kmq = jnp.concatenate([km, kr[:, :BLOCK]], axis=1)
    vmq = jnp.concatenate([vm, vr[:, :BLOCK]], axis=1)
    qmg = qm.reshape(Bn, N_META, KV, G, hd)
    s_m = jnp.einsum('bqkgd,bskd->bkgqs', qmg, kmq).astype(jnp.float32) * scale
    kpos = jnp.arange(N_META + BLOCK)
    qpos = jnp.arange(N_META)
    valid_m = (jnp.abs(kpos[None, :] - qpos[:, None]) <= WINDOW) | (kpos[None, :] < N_META)
    s_m = jnp.where(valid_m[None, None, None], s_m, NEG_INF)
    s_ms = jnp.broadcast_to(sink_g[None, :, :, None, None], s_m.shape[:-1] + (1,))
    p_m = jax.nn.softmax(jnp.concatenate([s_m, s_ms], axis=-1), axis=-1)[..., :-1].astype(v.dtype)
    o_meta = jnp.einsum('bkgqs,bskd->bqkgd', p_m, vmq).reshape(Bn, N_META, H, hd)
    return jnp.concatenate([o_meta, o_real], axis=1)


def _dense_block_attention(q, k, v):
    Bn, L, H, hd = q.shape
    KV = k.shape[2]
    G = H // KV
    S = L - N_META
    nb = S // BLOCK
    scale = hd ** -0.5

    def attend(qblk):
        s = jnp.einsum('bqkgd,bskd->bkgqs', qblk, k).astype(jnp.float32) * scale
        p = jax.nn.softmax(s, axis=-1).astype(v.dtype)
        return jnp.einsum('bkgqs,bskd->bqkgd', p, v)

    o_meta = attend(q[:, :N_META].reshape(Bn, N_META, KV, G, hd))
    qr = q[:, N_META:].reshape(Bn, nb, BLOCK, KV, G, hd).transpose(1, 0, 2, 3, 4, 5)
    o_real = lax.map(attend, qr)
    o_real = o_real.transpose(1, 0, 2, 3, 4, 5).reshape(Bn, S, H, hd)
    return jnp.concatenate([o_meta.reshape(Bn, N_META, H, hd), o_real], axis=1)


def _layer(h, w_in, sink, q_gain, k_gain, w_ba, w_bb, w_o, ln_g, ln_b, ang_a, row, col):
    Bn, L, _ = h.shape
    proj = h @ w_in
    cuts = [int(c) for c in np.cumsum(IN_SPLITS)[:-1]]
    qa, ka, va, za, qb, kb, vb, zb, ga, gb = jnp.split(proj, cuts, axis=-1)

    qa = _rotate(qa.reshape(Bn, L, A_HEADS, HEAD_DIM), ang_a)
    ka = _rotate(ka.reshape(Bn, L, A_KV, HEAD_DIM), ang_a)
    va = va.reshape(Bn, L, A_KV, HEAD_DIM)
    ya = _window_attention(qa, ka, va, sink).reshape(Bn, L, A_WIDTH) * jax.nn.silu(za)
    ua = ya @ w_ba

    qb = _axial_rope(_rms_norm(qb.reshape(Bn, L, B_HEADS, HEAD_DIM), q_gain), row, col)
    kb = _axial_rope(_rms_norm(kb.reshape(Bn, L, B_KV, HEAD_DIM), k_gain), row, col)
    vb = vb.reshape(Bn, L, B_KV, HEAD_DIM)
    yb = _dense_block_attention(qb, kb, vb).reshape(Bn, L, B_WIDTH) * jax.nn.silu(zb)
    ub = yb @ w_bb

    merged = jax.nn.sigmoid(ga) * ua + jax.nn.sigmoid(gb) * ub
    out = merged @ w_o
    return _layer_norm(ALPHA * h + out, ln_g, ln_b)


def _encode(x, meta_tokens, w_in, attn_a_sink, q_norm_b, k_norm_b, w_branch_a, w_branch_b,
            w_out, ln_gain, ln_bias):
    Bn, S, D = x.shape
    ROWS = S // GRID_W
    L = S + N_META
    meta = jnp.broadcast_to(meta_tokens.astype(x.dtype)[None], (Bn, N_META, D))
    h = jnp.concatenate([meta, x], axis=1)
    ang_a = _rope_angles(jnp.arange(L), HEAD_DIM)
    meta_pos = jnp.arange(N_META) - N_META
    row = jnp.concatenate([meta_pos, jnp.repeat(jnp.arange(ROWS), GRID_W)])
    col = jnp.concatenate([meta_pos, jnp.tile(jnp.arange(GRID_W), ROWS)])
    for l in range(DEPTH):
        h = _layer(h, w_in[l], attn_a_sink[l], q_norm_b[l], k_norm_b[l], w_branch_a[l],
                   w_branch_b[l], w_out[l], ln_gain[l], ln_bias[l], ang_a, row, col)
    return h[:, N_META:]


def setup_inputs(seed: int = 0) -> dict:
    key = jax.random.key(seed)
    ks = jax.random.split(key, 13)
    f32 = jnp.float32
    return {
        "x_prompt": jax.random.normal(ks[0], (BATCH, SEQ, D_MODEL), f32),
        "x_sample": jax.random.normal(ks[1], (DEC_BATCH, DEC_SEQ, D_MODEL), f32),
        "meta_tokens": jax.random.normal(ks[2], (N_META, D_MODEL), f32),
        "w_in": jax.random.normal(ks[3], (DEPTH, D_MODEL, D_IN), f32) * D_MODEL ** -0.5,
        "attn_a_sink": jax.random.normal(ks[4], (DEPTH, A_HEADS), f32) * 0.5,
        "q_norm_b": 1.0 + 0.02 * jax.random.normal(ks[5], (DEPTH, HEAD_DIM), f32),
        "k_norm_b": 1.0 + 0.02 * jax.random.normal(ks[6], (DEPTH, HEAD_DIM), f32),
        "w_branch_a": jax.random.normal(ks[7], (DEPTH, A_WIDTH, D_MODEL), f32) * (A_WIDTH ** -0.5 * BETA),
        "w_branch_b": jax.random.normal(ks[8], (DEPTH, B_WIDTH, D_MODEL), f32) * (B_WIDTH ** -0.5 * BETA),
        "w_out": jax.random.normal(ks[9], (DEPTH, D_MODEL, D_MODEL), f32) * (D_MODEL ** -0.5 * BETA),
        "ln_gain": 1.0 + 0.02 * jax.random.normal(ks[10], (DEPTH, D_MODEL), f32),
        "ln_bias": 0.02 * jax.random.normal(ks[11], (DEPTH, D_MODEL), f32),
    }


def reference(x_prompt, x_sample, meta_tokens, w_in, attn_a_sink, q_norm_b, k_norm_b,
              w_branch_a, w_branch_b, w_out, ln_gain, ln_bias):
    y_prompt = _encode(x_prompt, meta_tokens, w_in, attn_a_sink, q_norm_b, k_norm_b,
                       w_branch_a, w_branch_b, w_out, ln_gain, ln_bias)
    y_sample = _encode(x_sample, meta_tokens, w_in, attn_a_sink, q_norm_b, k_norm_b,
                       w_branch_a, w_branch_b, w_out, ln_gain, ln_bias)
    return (y_prompt, y_sample)
```

```python
import numpy as np
import ml_dtypes
from contextlib import ExitStack

import concourse.bass as bass
import concourse.mybir as mybir
from concourse.bass_utils import run_bass_kernel_spmd

F32 = mybir.dt.float32
BF16 = mybir.dt.bfloat16
AF = mybir.ActivationFunctionType
ALU = mybir.AluOpType

D_MODEL = 1024
KC = 8
HD = 64
N_META = 16
A_HEADS = 8
GRID_W = 64
ROPE_THETA = 10000.0
LN_EPS = 1e-5
RMS_EPS = 1e-6
ALPHA = 2.0 ** 0.25
NEG = -30000.0
N_CORES = 8
UNITS_PER_CORE = 3
NSLAB = 13
ROPE_OFFLOAD = False
ROPE_ENG = "pool"
COLRANGE = True
(S_QA, S_QB, S_ZA, S_ZB, S_GA0, S_GA1, S_GB0, S_GB1, S_WBA, S_WBB, S_WO0, S_WO1, S_WK) = range(NSLAB)

COMPUTE = ("pe", "act", "dve", "pool")
N_DMA_SEMS = 24


class Buf:
    __slots__ = ("name", "w", "r", "rd")

    def __init__(self, name):
        self.name = name
        self.w = None
        self.r = {}
        self.rd = []


class Prog:
    def __init__(self):
        self.ins = []
        self.order = {e: [] for e in COMPUTE + ("sp",)}

    def op(self, eng, fn, reads=(), writes=(), dma=False):
        i = len(self.ins)
        deps = set()
        for b in reads:
            if b.w is not None:
                deps.add(b.w)
        for b in writes:
            if b.w is not None:
                deps.add(b.w)
            deps.update(b.r.values())
            deps.update(b.rd)
        self.ins.append(dict(eng=eng, fn=fn, deps=deps, dma=dma))
        self.order[eng].append(i)
        for b in reads:
            if dma:
                b.rd.append(i)
            else:
                b.r[eng] = i
        for b in writes:
            b.w = i
            b.r = {}
            b.rd = []
        return i

    def finalize(self, nc, es):
        ins = self.ins
        dma_ids = [i for i, x in enumerate(ins) if x["dma"]]
        for k, i in enumerate(dma_ids):
            if k >= N_DMA_SEMS:
                ins[i]["deps"].add(dma_ids[k - N_DMA_SEMS])
        needed = set()
        for i, x in enumerate(ins):
            for d in x["deps"]:
                if ins[d]["eng"] == "pe" and x["eng"] == "pe" and not ins[d]["dma"] and not x["dma"]:
                    continue
                needed.add(d)
        sems = {e: es.enter_context(nc.semaphore("s_" + e)) for e in COMPUTE}
        dsems = [es.enter_context(nc.semaphore("s_dma%d" % k)) for k in range(N_DMA_SEMS)]
        sig = {}
        cnt = {e: 0 for e in COMPUTE}
        for i, x in enumerate(ins):
            if x["dma"]:
                continue
            if i in needed:
                cnt[x["eng"]] += 1
                sig[i] = (sems[x["eng"]], cnt[x["eng"]], 1)
        for k, i in enumerate(dma_ids):
            sig[i] = (dsems[k % N_DMA_SEMS], 16 * (k // N_DMA_SEMS + 1), 16)
        self.final_dma = {}
        for k, i in enumerate(dma_ids):
            self.final_dma[k % N_DMA_SEMS] = 16 * (k // N_DMA_SEMS + 1)
        thunks = {e: [] for e in self.order}
        for e, lst in self.order.items():
            waited = {}
            for i in lst:
                x = ins[i]
                for d in sorted(x["deps"]):
                    if d not in sig:
                        continue
                    sem, val, _ = sig[d]
                    key = id(sem)
                    if waited.get(key, 0) >= val:
                        continue
                    waited[key] = val
                    thunks[e].append(("w", sem, val))
                thunks[e].append(("i", x["fn"], sig.get(i)))
        self.thunks = thunks
        self.dsems = dsems

    @staticmethod
    def run(eng, lst):
        for t in lst:
            if t[0] == "w":
                eng.wait_ge(t[1], t[2])
            else:
                inst = t[1](eng)
                if t[2] is not None:
                    inst.then_inc(t[2][0], t[2][2])


def build_program(NT, NU, dbg=False):
    NBLK = NT // 128
    NCH = NT // 512
    LK = 2 * NT + N_META
    NKT = 2 * NBLK + 1
    NG = 2 * NCH + 1
    META_KT = 2 * NBLK

    nc = bass.Bass("TRN2", target_bir_lowering=False)
    P = Prog()

    def dram(name, shape, dt, kind):
        return nc.dram_tensor(name, shape, dt, kind=kind).ap()

    xT = dram("xT", [NU, D_MODEL, LK], F32, "ExternalInput")
    xq = dram("xq", [NU, NT, D_MODEL], F32, "ExternalInput")
    tabs = dram("tabs", [NU, 4, 128, LK], F32, "ExternalInput")
    masks = dram("masks", [NU, 8, 128, 512], BF16, "ExternalInput")
    wsrc = dram("wsrc", [NSLAB, 128, KC, 512], F32, "ExternalInput")
    cst = dram("cst", [128, 128 + 6], F32, "ExternalInput")
    identd = dram("ident", [128, 128], BF16, "ExternalInput")
    lngb = dram("lngb", [2, 128, D_MODEL], F32, "ExternalInput")
    sinkd = dram("sink", [1, A_HEADS], F32, "ExternalInput")
    yout = dram("y", [NU, NT, D_MODEL], F32, "ExternalOutput")
    wsc = dram("wsc", [NSLAB, 128, KC, 512], BF16, "Internal")
    if dbg:
        d_ka = dram("d_ka", [128, LK], BF16, "ExternalOutput")
        d_kb = dram("d_kb", [128, LK], BF16, "ExternalOutput")
        d_va = dram("d_va", [128, NKT * 130], BF16, "ExternalOutput")
        d_vb = dram("d_vb", [128, NKT * 130], BF16, "ExternalOutput")
        d_qm = dram("d_qm", [128, 8 * 512], BF16, "ExternalOutput")
        d_sz = dram("d_sz", [128, 8 * 512], BF16, "ExternalOutput")
        d_yt = dram("d_yt", [128, 8 * 512], BF16, "ExternalOutput")
        d_mg = dram("d_mg", [128, 8 * 512], BF16, "ExternalOutput")

    with ExitStack() as es:
        def sb(name, shape, dt):
            return es.enter_context(nc.sbuf_tensor(name, shape, dt))

        KA = sb("KA", [128, LK], BF16)
        KB = sb("KB", [128, LK], BF16)
        VA = sb("VA", [128, NKT, 2, 65], BF16)
        VB = sb("VB", [128, NKT, 2, 65], BF16)
        NW = 4
        WR = sb("WR", [128, NW, KC, 512], BF16)
        NXS = 4
        XS = sb("XS", [128, NXS, 512], F32)
        XB = sb("XB", [128, 2, KC, 512], BF16)
        XC = sb("XC", [128, 512], F32) if ROPE_OFFLOAD else None
        TB = sb("TB", [128, 4, 512], F32)
        GT = sb("GT", [128, 2, 512], F32)
        MSK = sb("MSK", [128, 8, 512], BF16)
        QM = sb("QM", [128, 8, 512], BF16)
        SZ = sb("SZ", [128, 8, 512], BF16)
        PT = sb("PT", [128, 2, 2, 512], BF16)
        YT = sb("YT", [128, 8, 512], BF16)
        RS = sb("RS", [128, 512], F32)
        TT = sb("TT", [128, 2, 512], F32)
        RR = sb("RR", [128, 2, 512], F32)
        T1, T2 = TT[:, 0, :], TT[:, 1, :]
        SQ, SD = RR[:, 0, :], RR[:, 1, :]
        RH = sb("RH", [128, 2, 512], BF16)
        RL = sb("RL", [128, 2, 512], BF16)
        SG = sb("SG", [128, 2, 512], BF16)
        NH2 = 3
        H2 = sb("H2", [128, NH2, D_MODEL], F32)
        LN = sb("LN", [128, 2, D_MODEL], F32)
        ST = sb("ST", [128, 3, 12], F32)
        MV = sb("MV", [128, 3, 4], F32)
        CS = sb("CS", [128, 128 + 6], F32)
        IDN = sb("IDN", [128, 128], BF16)
        ONES = sb("ONES", [128, 64], BF16)
        ESK = sb("ESK", [128, A_HEADS], F32)
        PS = es.enter_context(nc.psum_tensor("PS", [128, 8, 512], F32))

        bKA = [Buf("KA%d" % g) for g in range(NG)]
        bKB = [Buf("KB%d" % g) for g in range(NG)]
        bVA = [Buf("VA%d" % g) for g in range(NG)]
        bVB = [Buf("VB%d" % g) for g in range(NG)]
        bWR = [Buf("WR%d" % i) for i in range(NW)]
        bXS = [Buf("XS%d" % i) for i in range(NXS)]
        bXBk = [[Buf("XB%d_%d" % (j, k)) for k in range(KC)] for j in range(2)]
        bXC = Buf("XC")
        bTB = Buf("TB")
        bGT = Buf("GT")
        bMSK = Buf("MSK")
        bQM = [Buf("QM%d" % i) for i in range(8)]
        bSZ = [Buf("SZ%d" % i) for i in range(8)]
        bPT = [Buf("PT%d" % i) for i in range(2)]
        bYT = [[Buf("YT%d_%d" % (i, h)) for h in range(2)] for i in range(8)]
        bRS = Buf("RS")
        bTT = [Buf("TT0"), Buf("TT1")]
        bRR = [Buf("RR0"), Buf("RR1")]
        bT1, bT2 = bTT
        bSQ, bSD = bRR
        bRH = [Buf("RH0"), Buf("RH1")]
        bRL = [Buf("RL0"), Buf("RL1")]
        bSG = [Buf("SG0"), Buf("SG1")]
        bH2 = [Buf("H2%d" % i) for i in range(3)]
        bST = [Buf("ST%d" % i) for i in range(3)]
        bMV = [Buf("MV%d" % i) for i in range(3)]
        bH2h = [[Buf("H2h%d_%d" % (i, j)) for j in range(2)] for i in range(3)]
        bLN, bCS, bIDN, bONES, bESK = Buf("LN"), Buf("CS"), Buf("IDN"), Buf("ONES"), Buf("ESK")
        bPS = [Buf("PS%d" % i) for i in range(8)]
        bWSC = [Buf("WSC%d" % i) for i in range(NSLAB)]
        bVinit = Buf("Vinit")

        st = dict(xs=0, bank=0, wslot=0, xb=0, h2=0, pair=0)

        def next_bank():
            b = st["bank"]
            st["bank"] = (b + 1) % 8
            return b

        def dma(out, in_, reads, writes):
            return P.op("sp", lambda e, o=out, i=in_: e.dma_start(out=o, in_=i), reads, writes, dma=True)

        dma(CS[:], cst[:, :], [], [bCS])
        dma(IDN[:], identd[:, :], [], [bIDN])
        dma(LN[:], lngb.rearrange("a p n -> p a n"), [], [bLN])
        dma(ESK[64:65, :], sinkd[:, :], [], [bESK])
        P.op("act", lambda e: e.activation(out=ESK[64:65, :], in_=ESK[64:65, :], func=AF.Exp), [bESK], [bESK])
        P.op("pool", lambda e: e.memset(ONES[:], 1.0), [], [bONES])
        P.op("pool", lambda e: e.memset(VA[:].rearrange("p a b c -> p (a b c)"), 1.0), [], [bVinit] + bVA)
        P.op("pool", lambda e: e.memset(VB[:].rearrange("p a b c -> p (a b c)"), 1.0), [], [bVinit] + bVB)
        BLK1 = CS[:, 0:128]

        def load_x(u, t0, n, use_act=False):
            st["xb"] ^= 1
            xb = st["xb"]
            for kc in range(KC):
                s = st["xs"]
                st["xs"] = (s + 1) % NXS
                dma(XS[:, s, 0:n], xT[u, kc * 128:(kc + 1) * 128, t0:t0 + n], [], [bXS[s]])
                if use_act and kc % 2 == 1:
                    P.op("act", lambda e, s=s, kc=kc, xb=xb: e.activation(out=XB[:, xb, kc, 0:n], in_=XS[:, s, 0:n], func=AF.Copy),
                         [bXS[s]], [bXBk[xb][kc]])
                else:
                    P.op("pool", lambda e, s=s, kc=kc, xb=xb: e.tensor_copy(out=XB[:, xb, kc, 0:n], in_=XS[:, s, 0:n]),
                         [bXS[s]], [bXBk[xb][kc]])

        def load_tabs(u, t0, n):
            dma(TB[:, :, 0:n], tabs[u, :, :, t0:t0 + n].rearrange("f p n -> p f n"), [], [bTB])

        def prologue_slab(s, avoid=None):
            slot = st["wslot"]
            if slot == avoid:
                slot = (slot + 1) % NW
            st["wslot"] = (slot + 1) % NW
            for kc in range(KC):
                xs = st["xs"]
                st["xs"] = (xs + 1) % NXS
                dma(XS[:, xs, :], wsrc[s, :, kc, :], [], [bXS[xs]])
                if kc % 3 != 2:
                    P.op("act", lambda e, xs=xs, kc=kc, slot=slot: e.activation(out=WR[:, slot, kc, :], in_=XS[:, xs, :], func=AF.Copy),
                         [bXS[xs]], [bWR[slot]])
                else:
                    P.op("pool", lambda e, xs=xs, kc=kc, slot=slot: e.tensor_copy(out=WR[:, slot, kc, :], in_=XS[:, xs, :]),
                         [bXS[xs]], [bWR[slot]])
            dma(wsc[s], WR[:, slot], [bWR[slot]], [bWSC[s]])

        pref = {}

        def prefetch_slab(s):
            slot = st["wslot"]
            st["wslot"] = (slot + 1) % NW
            dma(WR[:, slot], wsc[s], [bWSC[s]], [bWR[slot]])
            pref.setdefault(s, []).append(slot)

        def get_slab(s):
            if not pref.get(s):
                prefetch_slab(s)
            return pref[s].pop(0)

        def mm(out, lhsT, rhs, start, stop, reads, writes):
            return P.op("pe", lambda e: e.matmul(out, lhsT=lhsT, rhs=rhs, start=start, stop=stop), reads, writes)

        def proj_fm(slot, c0, n, bank, xb=None):
            if xb is None:
                xb = st["xb"]
            for kc in range(KC):
                mm(PS[:, bank, 0:n], WR[:, slot, kc, c0:c0 + 128], XB[:, xb, kc, 0:n], kc == 0, kc == KC - 1,
                   [bWR[slot], bXBk[xb][kc]], [bPS[bank]])

        def rope(bank, n, Ctab, Stab, tabbufs, dst, dstbufs, rstd=False):
            ps = PS[:, bank, 0:n]
            P.op("dve", lambda e: e.tensor_tensor(out=T1[:, 0:n], in0=ps, in1=Ctab, op=ALU.mult),
                 [bPS[bank]] + tabbufs, [bT1])
            if ROPE_OFFLOAD:
                for q in range(4):
                    src = (q ^ 1) * 32
                    P.op("act", lambda e, q=q, src=src: e.activation(
                        out=XC[q * 32:(q + 1) * 32, 0:n], in_=PS[src:src + 32, bank, 0:n], func=AF.Copy),
                        [bPS[bank]], [bXC])
                P.op(ROPE_ENG, lambda e: e.tensor_tensor(out=T2[:, 0:n], in0=XC[:, 0:n], in1=Stab, op=ALU.mult),
                     [bXC] + tabbufs, [bT2])
            else:
                for q in range(4):
                    src = (q ^ 1) * 32
                    P.op("dve", lambda e, q=q, src=src: e.tensor_tensor(
                        out=T2[q * 32:(q + 1) * 32, 0:n], in0=PS[src:src + 32, bank, 0:n],
                        in1=Stab[q * 32:(q + 1) * 32, :], op=ALU.mult),
                        [bPS[bank]] + tabbufs, [bT2])
            if not rstd:
                P.op("dve", lambda e: e.tensor_tensor(out=dst, in0=T1[:, 0:n], in1=T2[:, 0:n], op=ALU.add),
                     [bT1, bT2], dstbufs)
            else:
                P.op("dve", lambda e: e.tensor_tensor(out=T1[:, 0:n], in0=T1[:, 0:n], in1=T2[:, 0:n], op=ALU.add),
                     [bT1, bT2], [bT1])
                P.op("dve", lambda e: e.tensor_tensor(out=dst, in0=T1[:, 0:n], in1=RS[:, 0:n], op=ALU.mult),
                     [bT1, bRS], dstbufs)

        def rms(bank, n):
            P.op("act", lambda e: e.activation(out=SQ[:, 0:n], in_=PS[:, bank, 0:n], func=AF.Square),
                 [bPS[bank]], [bSQ])
            b2 = next_bank()
            mm(PS[:, b2, 0:n], BLK1, SQ[:, 0:n], True, True, [bCS, bSQ], [bPS[b2]])
            P.op("act", lambda e: e.activation(out=SD[:, 0:n], in_=PS[:, b2, 0:n], func=AF.Ln,
                                               bias=CS[:, 132:133], scale=1.0 / HD),
                 [bPS[b2], bCS], [bSD])
            P.op("act", lambda e: e.activation(out=RS[:, 0:n], in_=SD[:, 0:n], func=AF.Exp, scale=-0.5), [bSD], [bRS])

        def gain_tabs(n, gcol):
            for j in range(2):
                P.op("pool", lambda e, j=j: e.tensor_scalar(
                    out=GT[:, j, 0:n], in0=TB[:, 2 + j, 0:n], scalar1=CS[:, 128 + gcol + j:128 + gcol + j + 1],
                    scalar2=1.0, op0=ALU.mult, op1=ALU.mult), [bTB, bCS], [bGT])

        SB = [(0, 1), (2, 3)]
        ACCS = [(4, 5), (6, 7)]

        pending = []

        def flush_pending():
            while pending:
                pending.pop(0)()

        def attention(KT, bKT, V, bV, tiles, qoff, sink):
            for c in range(4):
                qc = qoff + c
                nst = len(tiles)
                ACC = ACCS[st["pair"] % 2]
                st["pair"] += 1

                def qk(s):
                    kt, nk, mi, c0, c1 = tiles[s]
                    banks = SB[s % 2]
                    g = min(kt // 4, NG - 1)
                    for h in range(2):
                        b = banks[h]
                        mm(PS[0:nk, b, c0:c1], KT[64 * h:64 * h + 64, kt * 128:kt * 128 + nk],
                           QM[64 * h:64 * h + 64, qc, c0:c1], True, mi is None, [bKT[g], bQM[qc]], [bPS[b]])
                        if mi is not None:
                            mm(PS[0:nk, b, c0:c1], IDN[:, 0:nk], MSK[:, mi, c0:c1], False, True, [bIDN, bMSK], [bPS[b]])

                def ex(s):
                    kt, nk, mi, c0, c1 = tiles[s]
                    banks = SB[s % 2]
                    P.op("act", lambda e: e.activation(
                        out=PT[0:nk, s % 2, :, c0:c1], in_=PS[0:nk, banks[0]:banks[1] + 1, c0:c1], func=AF.Exp,
                        scale=HD ** -0.5), [bPS[banks[0]], bPS[banks[1]]], [bPT[s % 2]])

                def pv(s):
                    kt, nk, mi, c0, c1 = tiles[s]
                    g = min(kt // 4, NG - 1)
                    for h in range(2):
                        mm(PS[0:65, ACC[h], c0:c1], V[0:nk, kt, h, :], PT[0:nk, s % 2, h, c0:c1], s == 0, s == nst - 1,
                           [bV[g], bPT[s % 2]], [bPS[ACC[h]]])

                qk(0)
                ex(0)
                for s in range(nst):
                    if s + 1 < nst:
                        qk(s + 1)
                        ex(s + 1)
                    pv(s)
                    if s == min(12, nst - 1):
                        flush_pending()
                for h in range(2):
                    base = 64 * h
                    acc = ACC[h]
                    head = c + 4 * h
                    P.op("dve", lambda e, h=h, base=base, acc=acc, qc=qc: e.tensor_tensor(
                        out=TT[base:base + 64, h, :], in0=PS[0:64, acc, :], in1=SZ[base:base + 64, qc, :], op=ALU.mult),
                        [bPS[acc], bSZ[qc]], [bTT[h]])
                    if sink:
                        P.op("dve", lambda e, h=h, acc=acc, head=head: e.tensor_scalar(
                            out=RR[64:65, h, :], in0=PS[64:65, acc, :], scalar1=ESK[64:65, head:head + 1], scalar2=None,
                            op0=ALU.add), [bPS[acc], bESK], [bRR[h]])
                    else:
                        P.op("dve", lambda e, h=h, acc=acc: e.tensor_copy(out=RR[64:65, h, :], in_=PS[64:65, acc, :]),
                             [bPS[acc]], [bRR[h]])
                for h in range(2):
                    P.op("dve", lambda e, h=h: e.reciprocal(out=RR[64:65, h, :], in_=RR[64:65, h, :]), [bRR[h]], [bRR[h]])
                    P.op("dve", lambda e, h=h: e.tensor_copy(out=RH[64:65, h, :], in_=RR[64:65, h, :]), [bRR[h]], [bRH[h]])
                    P.op("dve", lambda e, h=h: e.tensor_tensor(out=RL[64:65, h, :], in0=RR[64:65, h, :], in1=RH[64:65, h, :],
                                                               op=ALU.subtract), [bRR[h], bRH[h]], [bRL[h]])

                def tail(qc=qc, ACC=ACC):
                    for h in range(2):
                        base = 64 * h
                        bcb = ACC[h]
                        mm(PS[base:base + 64, bcb, :], ONES[64:65, 0:64], RH[64:65, h, :], True, False, [bONES, bRH[h]], [bPS[bcb]])
                        mm(PS[base:base + 64, bcb, :], ONES[64:65, 0:64], RL[64:65, h, :], False, True, [bONES, bRL[h]], [bPS[bcb]])
                        P.op("dve", lambda e, h=h, base=base, qc=qc, bcb=bcb: e.tensor_tensor(
                            out=YT[base:base + 64, qc, :], in0=TT[base:base + 64, h, :], in1=PS[base:base + 64, bcb, :],
                            op=ALU.mult), [bTT[h], bPS[bcb]], [bYT[qc][h]])
                pending.append(tail)

        prologue_slab(S_WK)
        todo_slabs = [s_ for s_ in range(NSLAB) if s_ != S_WK]

        for u in range(NU):
            dma(MSK[:], masks[u].rearrange("m p n -> p m n"), [], [bMSK])
            wk = get_slab(S_WK)
            load_x(u, 0, 512, use_act=True)
            for g in range(NG):
                t0 = g * 512
                n = 512 if g < NG - 1 else N_META
                kxb = st["xb"]
                if g + 1 < NG:
                    load_x(u, t0 + 512, 512 if g + 1 < NG - 1 else N_META, use_act=True)
                load_tabs(u, t0, n)
                b = next_bank()
                proj_fm(wk, 0, n, b, kxb)
                rope(b, n, TB[:, 0, 0:n], TB[:, 1, 0:n], [bTB], KA[:, t0:t0 + n], [bKA[g]])
                b = next_bank()
                proj_fm(wk, 128, n, b, kxb)
                gain_tabs(n, 2)
                rms(b, n)
                rope(b, n, GT[:, 0, 0:n], GT[:, 1, 0:n], [bGT], KB[:, t0:t0 + n], [bKB[g]], rstd=True)
                for tt in range((n + 127) // 128):
                    m = min(128, n - tt * 128)
                    kt = g * 4 + tt
                    b = next_bank()
                    for kc in range(KC):
                        mm(PS[0:m, b, 0:256], XB[:, kxb, kc, tt * 128:tt * 128 + m], WR[:, wk, kc, 256:512], kc == 0, kc == KC - 1,
                           [bWR[wk], bXBk[kxb][kc]], [bPS[b]])
                    P.op("act", lambda e, m=m, kt=kt, b=b: e.activation(
                        out=VA[0:m, kt, :, 0:64], in_=PS[0:m, b, 0:128].rearrange("p (a d) -> p a d", a=2), func=AF.Copy),
                        [bPS[b], bVinit], [bVA[g]])
                    P.op("act", lambda e, m=m, kt=kt, b=b: e.activation(
                        out=VB[0:m, kt, :, 0:64], in_=PS[0:m, b, 128:256].rearrange("p (a d) -> p a d", a=2), func=AF.Copy),
                        [bPS[b], bVinit], [bVB[g]])
                if todo_slabs:
                    prologue_slab(todo_slabs.pop(0), avoid=wk)
            while todo_slabs:
                prologue_slab(todo_slabs.pop(0))

            if dbg and u == 0:
                dma(d_ka[:, :], KA[:], bKA, [])
                dma(d_kb[:, :], KB[:], bKB, [])
                dma(d_va[:, :], VA[:].rearrange("p a b c -> p (a b c)"), bVA, [])
                dma(d_vb[:, :], VB[:].rearrange("p a b c -> p (a b c)"), bVB, [])
            for ci in range(NCH):
                t0 = ci * 512
                if ci == 0:
                    load_x(u, t0, 512)
                    load_tabs(u, t0, 512)
                cur_xb = st["xb"]
                sl = get_slab(S_QA)
                for c in range(4):
                    b = next_bank()
                    proj_fm(sl, c * 128, 512, b)
                    rope(b, 512, TB[:, 0, :], TB[:, 1, :], [bTB], QM[:, c, :], [bQM[c]])
                sl = get_slab(S_QB)
                gain_tabs(512, 0)
                for c in range(4):
                    b = next_bank()
                    proj_fm(sl, c * 128, 512, b)
                    rms(b, 512)
                    rope(b, 512, GT[:, 0, :], GT[:, 1, :], [bGT], QM[:, 4 + c, :], [bQM[4 + c]], rstd=True)
                for off, sid in ((0, S_ZA), (4, S_ZB)):
                    sl = get_slab(sid)
                    for c in range(4):
                        b = next_bank()
                        proj_fm(sl, c * 128, 512, b)
                        P.op("act", lambda e, b=b, c=c, off=off: e.activation(out=SZ[:, off + c, :], in_=PS[:, b, :], func=AF.Silu),
                             [bPS[b]], [bSZ[off + c]])
                if dbg and u == 0 and ci == 0:
                    dma(d_qm[:, :], QM[:].rearrange("p a b -> p (a b)"), bQM, [])
                    dma(d_sz[:, :], SZ[:].rearrange("p a b -> p (a b)"), bSZ, [])
                if ci + 1 < NCH:
                    load_x(u, t0 + 512, 512)
                    load_tabs(u, t0 + 512, 512)
                tiles = [(META_KT, N_META, None, 0, 512)]
                for r in range(-1, 5):
                    t = 4 * ci + r
                    c0, c1 = (128 * max(0, r - 1), 128 * (min(3, r + 1) + 1)) if COLRANGE else (0, 512)
                    if 0 <= t < NBLK:
                        tiles.append((t, 128, r + 1, c0, c1))
                    elif t == -1:
                        tiles.append((2 * NBLK - 1, 128, 6, c0, c1))
                    elif t == NBLK:
                        tiles.append((NBLK, 128, 7, c0, c1))
                attention(KA, bKA, VA, bVA, tiles, 0, True)
                tiles = [(t, 128, None, 0, 512) for t in range(2 * NBLK)] + [(META_KT, N_META, None, 0, 512)]
                attention(KB, bKB, VB, bVB, tiles, 4, False)
                flush_pending()
                if dbg and u == 0 and ci == 0:
                    dma(d_yt[:, :], YT[:].rearrange("p a b -> p (a b)"), [x for p_ in bYT for x in p_], [])
                for hh in range(2):
                    sga = get_slab(S_GA0 + hh)
                    sgb = get_slab(S_GB0 + hh)
                    if hh == 0:
                        swa = get_slab(S_WBA)
                        swb = get_slab(S_WBB)
                        WBAv = WR[:, swa].rearrange("p (k h) c -> p k (h c)", h=2)
                        WBBv = WR[:, swb].rearrange("p (k h) c -> p k (h c)", h=2)
                    for jj in range(4):
                        j = hh * 4 + jj
                        for br, (sg, Wv, sw, yoff) in enumerate(((sga, WBAv, swa, 0), (sgb, WBBv, swb, 4))):
                            b = next_bank()
                            proj_fm(sg, jj * 128, 512, b, cur_xb)
                            P.op("act", lambda e, b=b, br=br: e.activation(out=SG[:, br, :], in_=PS[:, b, :], func=AF.Sigmoid),
                                 [bPS[b]], [bSG[br]])
                            b2 = next_bank()
                            for kc in range(4):
                                mm(PS[:, b2, :], Wv[:, kc, j * 128:(j + 1) * 128], YT[:, yoff + kc, :], kc == 0, kc == 3,
                                   [bWR[sw], bYT[yoff + kc][0], bYT[yoff + kc][1]], [bPS[b2]])
                            tdst, tb = (T1, bT1) if br == 0 else (T2, bT2)
                            P.op("dve", lambda e, b2=b2, br=br, tdst=tdst: e.tensor_tensor(
                                out=tdst[:, :], in0=PS[:, b2, :], in1=SG[:, br, :], op=ALU.mult), [bPS[b2], bSG[br]], [tb])
                        P.op("dve", lambda e, j=j: e.tensor_tensor(out=QM[:, j, :], in0=T1[:, :], in1=T2[:, :], op=ALU.add),
                             [bT1, bT2], [bQM[j]])
                if dbg and u == 0 and ci == 0:
                    dma(d_mg[:, :], QM[:].rearrange("p a b -> p (a b)"), bQM, [])
                wo = [get_slab(S_WO0), get_slab(S_WO1)]
                if ci + 1 < NCH:
                    prefetch_slab(S_QA)
                    prefetch_slab(S_QB)
                elif u + 1 < NU:
                    prefetch_slab(S_WK)
                h2slots = {}

                def h2_load(blk_):
                    hs_ = st["h2"]
                    st["h2"] = (hs_ + 1) % NH2
                    r0_ = t0 + blk_ * 128
                    dma(H2[:, hs_, :], xq[u, r0_:r0_ + 128, :], [], [bH2[hs_], bH2h[hs_][0], bH2h[hs_][1]])
                    h2slots[blk_] = hs_
                for blk in range(NH2 - 1):
                    h2_load(blk)
                for blk in range(4):
                    if blk + NH2 - 1 < 4:
                        h2_load(blk + NH2 - 1)
                    hs = h2slots[blk]
                    r0 = t0 + blk * 128
                    for hc in range(2):
                        b = next_bank()
                        for kc in range(KC):
                            mm(PS[:, b, :], QM[:, kc, blk * 128:(blk + 1) * 128], WR[:, wo[hc], kc, :], kc == 0, kc == KC - 1,
                               [bQM[kc], bWR[wo[hc]]], [bPS[b]])
                        P.op("dve", lambda e, hs=hs, hc=hc, b=b: e.scalar_tensor_tensor(
                            out=H2[:, hs, hc * 512:(hc + 1) * 512], in0=H2[:, hs, hc * 512:(hc + 1) * 512], scalar=ALPHA,
                            in1=PS[:, b, :], op0=ALU.mult, op1=ALU.add), [bH2[hs], bPS[b]], [bH2[hs]])
                        P.op("dve", lambda e, hs=hs, hc=hc: e.bn_stats(out=ST[:, hs, hc * 6:(hc + 1) * 6],
                                                                      in_=H2[:, hs, hc * 512:(hc + 1) * 512]),
                             [bH2[hs]], [bST[hs]])
                    P.op("dve", lambda e, hs=hs: e.bn_aggr(out=MV[:, hs, 0:2], in_=ST[:, hs, :]), [bST[hs]], [bMV[hs]])
                    P.op("act", lambda e, hs=hs: e.activation(out=MV[:, hs, 2:3], in_=MV[:, hs, 1:2], func=AF.Ln,
                                                              bias=CS[:, 133:134], scale=1.0), [bMV[hs], bCS], [bMV[hs]])
                    P.op("act", lambda e, hs=hs: e.activation(out=MV[:, hs, 3:4], in_=MV[:, hs, 2:3], func=AF.Exp, scale=-0.5),
                         [bMV[hs]], [bMV[hs]])
                    P.op("dve", lambda e, hs=hs: e.tensor_scalar(
                        out=H2[:, hs, :], in0=H2[:, hs, :], scalar1=MV[:, hs, 0:1], scalar2=MV[:, hs, 3:4],
                        op0=ALU.subtract, op1=ALU.mult), [bH2[hs], bMV[hs]], [bH2[hs]])
                    for eng_, c0_, c1_ in (("pool", 0, 512), ("dve", 512, 1024)):
                        P.op(eng_, lambda e, hs=hs, c0_=c0_, c1_=c1_: e.tensor_tensor(
                            out=H2[:, hs, c0_:c1_], in0=H2[:, hs, c0_:c1_], in1=LN[:, 0, c0_:c1_], op=ALU.mult),
                            [bH2[hs], bLN], [bH2h[hs][c0_ // 512]])
                        P.op(eng_, lambda e, hs=hs, c0_=c0_, c1_=c1_: e.tensor_tensor(
                            out=H2[:, hs, c0_:c1_], in0=H2[:, hs, c0_:c1_], in1=LN[:, 1, c0_:c1_], op=ALU.add),
                            [bH2h[hs][c0_ // 512], bLN], [bH2h[hs][c0_ // 512]])
                    dma(yout[u, r0:r0 + 128, :], H2[:, hs, :], [bH2[hs], bH2h[hs][0], bH2h[hs][1]], [])

        P.finalize(nc, es)
        block = es.enter_context(nc.Block())

        @block.sync
        def _(e):
            Prog.run(e, P.thunks["sp"])
            for k, v in P.final_dma.items():
                e.wait_ge(P.dsems[k], v)

        @block.tensor
        def _(e):
            Prog.run(e, P.thunks["pe"])

        @block.scalar
        def _(e):
            Prog.run(e, P.thunks["act"])

        @block.vector
        def _(e):
            Prog.run(e, P.thunks["dve"])

        @block.gpsimd
        def _(e):
            Prog.run(e, P.thunks["pool"])

    return nc


PERM_B = np.array(list(range(0, 16)) + list(range(32, 48)) + list(range(16, 32)) + list(range(48, 64)))
OFF = dict(qa=0, ka=512, va=640, za=768, qb=1280, kb=1792, vb=1920, zb=2048, ga=2560, gb=3584)


def _head_cols(off, c, perm):
    return np.concatenate([off + c * 64 + perm, off + (4 + c) * 64 + perm])


def make_slabs(w_in, w_ba, w_bb, w_out):
    nat = np.arange(64)
    cols = {}
    cols[S_QA] = np.concatenate([_head_cols(OFF["qa"], c, nat) for c in range(4)])
    cols[S_QB] = np.concatenate([_head_cols(OFF["qb"], c, PERM_B) for c in range(4)])
    cols[S_ZA] = np.concatenate([_head_cols(OFF["za"], c, nat) for c in range(4)])
    cols[S_ZB] = np.concatenate([_head_cols(OFF["zb"], c, nat) for c in range(4)])
    cols[S_GA0] = OFF["ga"] + np.arange(0, 512)
    cols[S_GA1] = OFF["ga"] + np.arange(512, 1024)
    cols[S_GB0] = OFF["gb"] + np.arange(0, 512)
    cols[S_GB1] = OFF["gb"] + np.arange(512, 1024)
    cols[S_WK] = np.concatenate([OFF["ka"] + np.arange(128),
                                 OFF["kb"] + PERM_B, OFF["kb"] + 64 + PERM_B,
                                 OFF["va"] + np.arange(128), OFF["vb"] + np.arange(128)])
    slabs = np.zeros((NSLAB, 128, KC, 512), np.float32)
    for s, cc in cols.items():
        slabs[s] = w_in[:, cc].reshape(KC, 128, 512).transpose(1, 0, 2)
    rowperm = np.concatenate([np.concatenate([c * 64 + nat, (4 + c) * 64 + nat]) for c in range(4)])
    for s, w in ((S_WBA, w_ba), (S_WBB, w_bb)):
        t = w[rowperm, :].reshape(4, 128, 1024).transpose(1, 0, 2)
        slabs[s] = t.reshape(128, 4, 2, 512).reshape(128, 8, 512)
    for hc, s in ((0, S_WO0), (1, S_WO1)):
        slabs[s] = w_out[:, hc * 512:(hc + 1) * 512].reshape(KC, 128, 512).transpose(1, 0, 2)
    return slabs


def make_tables(NT, S, hf):
    own = np.arange(hf * NT, (hf + 1) * NT)
    oth = np.arange((1 - hf) * NT, (2 - hf) * NT)
    real = np.concatenate([own, oth])
    metai = np.arange(N_META)
    posA = np.concatenate([N_META + real, metai]).astype(np.float32)
    rowB = np.concatenate([real // GRID_W, metai - N_META]).astype(np.float32)
    colB = np.concatenate([real % GRID_W, metai - N_META]).astype(np.float32)
    LK = 2 * NT + N_META
    inv32 = (ROPE_THETA ** (-np.arange(0, 64, 2, dtype=np.float32) / 64)).astype(np.float32)
    inv16 = (ROPE_THETA ** (-np.arange(0, 32, 2, dtype=np.float32) / 32)).astype(np.float32)
    tab = np.zeros((4, 128, LK), np.float32)
    for p in range(128):
        d = p % 64
        angA = posA * inv32[d % 32]
        tab[0, p] = np.cos(angA)
        tab[1, p] = np.sin(angA) * (-1.0 if d < 32 else 1.0)
        od = PERM_B[d]
        pos = rowB if od < 32 else colB
        angB = pos * inv16[od % 16]
        tab[2, p] = np.cos(angB)
        tab[3, p] = np.sin(angB) * (-1.0 if (od % 32) < 16 else 1.0)
    return tab


def make_masks(hf):
    m = np.zeros((8, 128, 512), np.float32)
    j = np.arange(128)[:, None]
    i = np.arange(128)[None, :]
    for r in range(-1, 5):
        for qb in range(4):
            rel = (r - qb) * 128 + j - i
            m[r + 1, :, qb * 128:(qb + 1) * 128] = np.where(np.abs(rel) <= 128, 0.0, NEG)
    m[6] = m[0] if hf == 1 else NEG
    m[7] = m[5] if hf == 0 else NEG
    return m.astype(ml_dtypes.bfloat16)


def make_consts(q_norm_b, k_norm_b):
    cs = np.zeros((128, 134), np.float32)
    cs[:, 132] = RMS_EPS
    cs[:, 133] = LN_EPS
    p = np.arange(128)
    cs[:, 0:128] = (p[:, None] // 64 == p[None, :] // 64).astype(np.float32)
    d = p % 64
    sw = np.where(d < 32, d + 32, d - 32)
    cs[:, 128] = q_norm_b[PERM_B[d]]
    cs[:, 129] = q_norm_b[PERM_B[sw]]
    cs[:, 130] = k_norm_b[PERM_B[d]]
    cs[:, 131] = k_norm_b[PERM_B[sw]]
    return cs


def unit_inputs(x_seq, meta_tokens, hf, NT):
    own = x_seq[hf * NT:(hf + 1) * NT]
    oth = x_seq[(1 - hf) * NT:(2 - hf) * NT]
    xT = np.ascontiguousarray(np.concatenate([own, oth, meta_tokens], axis=0).T)
    return xT, np.ascontiguousarray(own)


_NC_CACHE = {}


def kernel(x_prompt, x_sample, meta_tokens, w_in, attn_a_sink, q_norm_b, k_norm_b,
           w_branch_a, w_branch_b, w_out, ln_gain, ln_bias):
    f = lambda a: np.asarray(a, dtype=np.float32)
    x_prompt, x_sample, meta_tokens = f(x_prompt), f(x_sample), f(meta_tokens)
    S = x_prompt.shape[1]
    NT = S // 2
    seqs = [x_prompt[b] for b in range(x_prompt.shape[0])] + [x_sample[b] for b in range(x_sample.shape[0])]
    n_units = 2 * len(seqs)
    NU = n_units // N_CORES
    assert NU * N_CORES == n_units
    slabs = make_slabs(f(w_in)[0], f(w_branch_a)[0], f(w_branch_b)[0], f(w_out)[0])
    cs = make_consts(f(q_norm_b)[0], f(k_norm_b)[0])
    ident = np.eye(128, dtype=np.float32).astype(ml_dtypes.bfloat16)
    lngb = np.stack([np.broadcast_to(f(ln_gain)[0][None, :], (128, D_MODEL)),
                     np.broadcast_to(f(ln_bias)[0][None, :], (128, D_MODEL))]).astype(np.float32)
    sink = f(attn_a_sink)[0][None, :]
    tabs_hf = [make_tables(NT, S, 0), make_tables(NT, S, 1)]
    masks_hf = [make_masks(0), make_masks(1)]
    in_maps = []
    for c in range(N_CORES):
        xTs, xqs, tbs, mks = [], [], [], []
        for j in range(NU):
            uid = c * NU + j
            s, hf = uid // 2, uid % 2
            xT_, xq_ = unit_inputs(seqs[s], meta_tokens, hf, NT)
            xTs.append(xT_)
            xqs.append(xq_)
            tbs.append(tabs_hf[hf])
            mks.append(masks_hf[hf])
        in_maps.append(dict(xT=np.stack(xTs), xq=np.stack(xqs), tabs=np.stack(tbs), masks=np.stack(mks),
                            wsrc=slabs, cst=cs, ident=ident, lngb=lngb, sink=sink))
    key = (NT, NU)
    if key not in _NC_CACHE:
        _NC_CACHE[key] = build_program(NT, NU)
    nc = _NC_CACHE[key]
    res = run_bass_kernel_spmd(nc, in_maps, core_ids=list(range(N_CORES)))
    outs = [np.zeros((S, D_MODEL), np.float32) for _ in seqs]
    for c in range(N_CORES):
        yc = np.asarray(res.results[c]["y"])
        for j in range(NU):
            uid = c * NU + j
            s, hf = uid // 2, uid % 2
            outs[s][hf * NT:(hf + 1) * NT] = yc[j]
    nb = x_prompt.shape[0]
    y_prompt = np.stack(outs[:nb]).astype(np.float32)
    y_sample = np.stack(outs[nb:]).astype(np.float32)
    return (y_prompt, y_sample)
```

```python
import numpy as np
import ml_dtypes
from contextlib import ExitStack

import concourse.bass as bass
import concourse.mybir as mybir
from concourse.bass_utils import run_bass_kernel_spmd

F32 = mybir.dt.float32
BF16 = mybir.dt.bfloat16
AF = mybir.ActivationFunctionType
ALU = mybir.AluOpType

D_MODEL = 1024
KC = 8
HD = 64
N_META = 16
A_HEADS = 8
GRID_W = 64
ROPE_THETA = 10000.0
LN_EPS = 1e-5
RMS_EPS = 1e-6
ALPHA = 2.0 ** 0.25
NEG = -30000.0
N_CORES = 8
UNITS_PER_CORE = 3
NSLAB = 13
ROPE_OFFLOAD = False
ROPE_ENG = "pool"
COLRANGE = True
(S_QA, S_QB, S_ZA, S_ZB, S_GA0, S_GA1, S_GB0, S_GB1, S_WBA, S_WBB, S_WO0, S_WO1, S_WK) = range(NSLAB)

COMPUTE = ("pe", "act", "dve", "pool")
N_DMA_SEMS = 24


class Buf:
    __slots__ = ("name", "w", "r", "rd")

    def __init__(self, name):
        self.name = name
        self.w = None
        self.r = {}
        self.rd = []


class Prog:
    def __init__(self):
        self.ins = []
        self.order = {e: [] for e in COMPUTE + ("sp",)}

    def op(self, eng, fn, reads=(), writes=(), dma=False):
        i = len(self.ins)
        deps = set()
        for b in reads:
            if b.w is not None:
                deps.add(b.w)
        for b in writes:
            if b.w is not None:
                deps.add(b.w)
            deps.update(b.r.values())
            deps.update(b.rd)
        self.ins.append(dict(eng=eng, fn=fn, deps=deps, dma=dma))
        self.order[eng].append(i)
        for b in reads:
            if dma:
                b.rd.append(i)
            else:
                b.r[eng] = i
        for b in writes:
            b.w = i
            b.r = {}
            b.rd = []
        return i

    def finalize(self, nc, es):
        ins = self.ins
        dma_ids = [i for i, x in enumerate(ins) if x["dma"]]
        for k, i in enumerate(dma_ids):
            if k >= N_DMA_SEMS:
                ins[i]["deps"].add(dma_ids[k - N_DMA_SEMS])
        needed = set()
        for i, x in enumerate(ins):
            for d in x["deps"]:
                if ins[d]["eng"] == "pe" and x["eng"] == "pe" and not ins[d]["dma"] and not x["dma"]:
                    continue
                needed.add(d)
        sems = {e: es.enter_context(nc.semaphore("s_" + e)) for e in COMPUTE}
        dsems = [es.enter_context(nc.semaphore("s_dma%d" % k)) for k in range(N_DMA_SEMS)]
        sig = {}
        cnt = {e: 0 for e in COMPUTE}
        for i, x in enumerate(ins):
            if x["dma"]:
                continue
            if i in needed:
                cnt[x["eng"]] += 1
                sig[i] = (sems[x["eng"]], cnt[x["eng"]], 1)
        for k, i in enumerate(dma_ids):
            sig[i] = (dsems[k % N_DMA_SEMS], 16 * (k // N_DMA_SEMS + 1), 16)
        self.final_dma = {}
        for k, i in enumerate(dma_ids):
            self.final_dma[k % N_DMA_SEMS] = 16 * (k // N_DMA_SEMS + 1)
        thunks = {e: [] for e in self.order}
        for e, lst in self.order.items():
            waited = {}
            for i in lst:
                x = ins[i]
                for d in sorted(x["deps"]):
                    if d not in sig:
                        continue
                    sem, val, _ = sig[d]
                    key = id(sem)
                    if waited.get(key, 0) >= val:
                        continue
                    waited[key] = val
                    thunks[e].append(("w", sem, val))
                thunks[e].append(("i", x["fn"], sig.get(i)))
        self.thunks = thunks
        self.dsems = dsems

    @staticmethod
    def run(eng, lst):
        for t in lst:
            if t[0] == "w":
                eng.wait_ge(t[1], t[2])
            else:
                inst = t[1](eng)
                if t[2] is not None:
                    inst.then_inc(t[2][0], t[2][2])


def build_program(NT, NU, dbg=False):
    NBLK = NT // 128
    NCH = NT // 512
    LK = 2 * NT + N_META
    NKT = 2 * NBLK + 1
    NG = 2 * NCH + 1
    META_KT = 2 * NBLK

    nc = bass.Bass("TRN2", target_bir_lowering=False)
    P = Prog()

    def dram(name, shape, dt, kind):
        return nc.dram_tensor(name, shape, dt, kind=kind).ap()

    xT = dram("xT", [NU, D_MODEL, LK], F32, "ExternalInput")
    xq = dram("xq", [NU, NT, D_MODEL], F32, "ExternalInput")
    tabs = dram("tabs", [NU, 4, 128, LK], F32, "ExternalInput")
    masks = dram("masks", [NU, 8, 128, 512], BF16, "ExternalInput")
    wsrc = dram("wsrc", [NSLAB, 128, KC, 512], F32, "ExternalInput")
    cst = dram("cst", [128, 128 + 6], F32, "ExternalInput")
    identd = dram("ident", [128, 128], BF16, "ExternalInput")
    lngb = dram("lngb", [2, 128, D_MODEL], F32, "ExternalInput")
    sinkd = dram("sink", [1, A_HEADS], F32, "ExternalInput")
    yout = dram("y", [NU, NT, D_MODEL], F32, "ExternalOutput")
    wsc = dram("wsc", [NSLAB, 128, KC, 512], BF16, "Internal")
    if dbg:
        d_ka = dram("d_ka", [128, LK], BF16, "ExternalOutput")
        d_kb = dram("d_kb", [128, LK], BF16, "ExternalOutput")
        d_va = dram("d_va", [128, NKT * 130], BF16, "ExternalOutput")
        d_vb = dram("d_vb", [128, NKT * 130], BF16, "ExternalOutput")
        d_qm = dram("d_qm", [128, 8 * 512], BF16, "ExternalOutput")
        d_sz = dram("d_sz", [128, 8 * 512], BF16, "ExternalOutput")
        d_yt = dram("d_yt", [128, 8 * 512], BF16, "ExternalOutput")
        d_mg = dram("d_mg", [128, 8 * 512], BF16, "ExternalOutput")

    with ExitStack() as es:
        def sb(name, shape, dt):
            return es.enter_context(nc.sbuf_tensor(name, shape, dt))

        KA = sb("KA", [128, LK], BF16)
        KB = sb("KB", [128, LK], BF16)
        VA = sb("VA", [128, NKT, 2, 65], BF16)
        VB = sb("VB", [128, NKT, 2, 65], BF16)
        NW = 4
        WR = sb("WR", [128, NW, KC, 512], BF16)
        NXS = 4
        XS = sb("XS", [128, NXS, 512], F32)
        XB = sb("XB", [128, 2, KC, 512], BF16)
        XC = sb("XC", [128, 512], F32) if ROPE_OFFLOAD else None
        TB = sb("TB", [128, 4, 512], F32)
        GT = sb("GT", [128, 2, 512], F32)
        MSK = sb("MSK", [128, 8, 512], BF16)
        QM = sb("QM", [128, 8, 512], BF16)
        SZ = sb("SZ", [128, 8, 512], BF16)
        PT = sb("PT", [128, 2, 2, 512], BF16)
        YT = sb("YT", [128, 8, 512], BF16)
        RS = sb("RS", [128, 512], F32)
        TT = sb("TT", [128, 2, 512], F32)
        RR = sb("RR", [128, 2, 512], F32)
        T1, T2 = TT[:, 0, :], TT[:, 1, :]
        SQ, SD = RR[:, 0, :], RR[:, 1, :]
        RH = sb("RH", [128, 2, 512], BF16)
        RL = sb("RL", [128, 2, 512], BF16)
        SG = sb("SG", [128, 2, 512], BF16)
        NH2 = 3
        H2 = sb("H2", [128, NH2, D_MODEL], F32)
        LN = sb("LN", [128, 2, D_MODEL], F32)
        ST = sb("ST", [128, 3, 12], F32)
        MV = sb("MV", [128, 3, 4], F32)
        CS = sb("CS", [128, 128 + 6], F32)
        IDN = sb("IDN", [128, 128], BF16)
        ONES = sb("ONES", [128, 64], BF16)
        ESK = sb("ESK", [128, A_HEADS], F32)
        PS = es.enter_context(nc.psum_tensor("PS", [128, 8, 512], F32))

        bKA = [Buf("KA%d" % g) for g in range(NG)]
        bKB = [Buf("KB%d" % g) for g in range(NG)]
        bVA = [Buf("VA%d" % g) for g in range(NG)]
        bVB = [Buf("VB%d" % g) for g in range(NG)]
        bWR = [Buf("WR%d" % i) for i in range(NW)]
        bXS = [Buf("XS%d" % i) for i in range(NXS)]
        bXBk = [[Buf("XB%d_%d" % (j, k)) for k in range(KC)] for j in range(2)]
        bXC = Buf("XC")
        bTB = Buf("TB")
        bGT = Buf("GT")
        bMSK = Buf("MSK")
        bQM = [Buf("QM%d" % i) for i in range(8)]
        bSZ = [Buf("SZ%d" % i) for i in range(8)]
        bPT = [Buf("PT%d" % i) for i in range(2)]
        bYT = [[Buf("YT%d_%d" % (i, h)) for h in range(2)] for i in range(8)]
        bRS = Buf("RS")
        bTT = [Buf("TT0"), Buf("TT1")]
        bRR = [Buf("RR0"), Buf("RR1")]
        bT1, bT2 = bTT
        bSQ, bSD = bRR
        bRH = [Buf("RH0"), Buf("RH1")]
        bRL = [Buf("RL0"), Buf("RL1")]
        bSG = [Buf("SG0"), Buf("SG1")]
        bH2 = [Buf("H2%d" % i) for i in range(3)]
        bST = [Buf("ST%d" % i) for i in range(3)]
        bMV = [Buf("MV%d" % i) for i in range(3)]
        bH2h = [[Buf("H2h%d_%d" % (i, j)) for j in range(2)] for i in range(3)]
        bLN, bCS, bIDN, bONES, bESK = Buf("LN"), Buf("CS"), Buf("IDN"), Buf("ONES"), Buf("ESK")
        bPS = [Buf("PS%d" % i) for i in range(8)]
        bWSC = [Buf("WSC%d" % i) for i in range(NSLAB)]
        bVinit = Buf("Vinit")

        st = dict(xs=0, bank=0, wslot=0, xb=0, h2=0, pair=0)

        def next_bank():
            b = st["bank"]
            st["bank"] = (b + 1) % 8
            return b

        def dma(out, in_, reads, writes):
            return P.op("sp", lambda e, o=out, i=in_: e.dma_start(out=o, in_=i), reads, writes, dma=True)

        dma(CS[:], cst[:, :], [], [bCS])
        dma(IDN[:], identd[:, :], [], [bIDN])
        dma(LN[:], lngb.rearrange("a p n -> p a n"), [], [bLN])
        dma(ESK[64:65, :], sinkd[:, :], [], [bESK])
        P.op("act", lambda e: e.activation(out=ESK[64:65, :], in_=ESK[64:65, :], func=AF.Exp), [bESK], [bESK])
        P.op("pool", lambda e: e.memset(ONES[:], 1.0), [], [bONES])
        P.op("pool", lambda e: e.memset(VA[:].rearrange("p a b c -> p (a b c)"), 1.0), [], [bVinit] + bVA)
        P.op("pool", lambda e: e.memset(VB[:].rearrange("p a b c -> p (a b c)"), 1.0), [], [bVinit] + bVB)
        BLK1 = CS[:, 0:128]

        def load_x(u, t0, n, use_act=False):
            st["xb"] ^= 1
            xb = st["xb"]
            for kc in range(KC):
                s = st["xs"]
                st["xs"] = (s + 1) % NXS
                dma(XS[:, s, 0:n], xT[u, kc * 128:(kc + 1) * 128, t0:t0 + n], [], [bXS[s]])
                if use_act and kc % 2 == 1:
                    P.op("act", lambda e, s=s, kc=kc, xb=xb: e.activation(out=XB[:, xb, kc, 0:n], in_=XS[:, s, 0:n], func=AF.Copy),
                         [bXS[s]], [bXBk[xb][kc]])
                else:
                    P.op("pool", lambda e, s=s, kc=kc, xb=xb: e.tensor_copy(out=XB[:, xb, kc, 0:n], in_=XS[:, s, 0:n]),
                         [bXS[s]], [bXBk[xb][kc]])

        def load_tabs(u, t0, n):
            dma(TB[:, :, 0:n], tabs[u, :, :, t0:t0 + n].rearrange("f p n -> p f n"), [], [bTB])

        def prologue_slab(s, avoid=None):
            slot = st["wslot"]
            if slot == avoid:
                slot = (slot + 1) % NW
            st["wslot"] = (slot + 1) % NW
            for kc in range(KC):
                xs = st["xs"]
                st["xs"] = (xs + 1) % NXS
                dma(XS[:, xs, :], wsrc[s, :, kc, :], [], [bXS[xs]])
                if kc % 3 != 2:
                    P.op("act", lambda e, xs=xs, kc=kc, slot=slot: e.activation(out=WR[:, slot, kc, :], in_=XS[:, xs, :], func=AF.Copy),
                         [bXS[xs]], [bWR[slot]])
                else:
                    P.op("pool", lambda e, xs=xs, kc=kc, slot=slot: e.tensor_copy(out=WR[:, slot, kc, :], in_=XS[:, xs, :]),
                         [bXS[xs]], [bWR[slot]])
            dma(wsc[s], WR[:, slot], [bWR[slot]], [bWSC[s]])

        pref = {}

        def prefetch_slab(s):
            slot = st["wslot"]
            st["wslot"] = (slot + 1) % NW
            dma(WR[:, slot], wsc[s], [bWSC[s]], [bWR[slot]])
            pref.setdefault(s, []).append(slot)

        def get_slab(s):
            if not pref.get(s):
                prefetch_slab(s)
            return pref[s].pop(0)

        def mm(out, lhsT, rhs, start, stop, reads, writes):
            return P.op("pe", lambda e: e.matmul(out, lhsT=lhsT, rhs=rhs, start=start, stop=stop), reads, writes)

        def proj_fm(slot, c0, n, bank, xb=None):
            if xb is None:
                xb = st["xb"]
            for kc in range(KC):
                mm(PS[:, bank, 0:n], WR[:, slot, kc, c0:c0 + 128], XB[:, xb, kc, 0:n], kc == 0, kc == KC - 1,
                   [bWR[slot], bXBk[xb][kc]], [bPS[bank]])

        def rope(bank, n, Ctab, Stab, tabbufs, dst, dstbufs, rstd=False):
            ps = PS[:, bank, 0:n]
            P.op("dve", lambda e: e.tensor_tensor(out=T1[:, 0:n], in0=ps, in1=Ctab, op=ALU.mult),
                 [bPS[bank]] + tabbufs, [bT1])
            if ROPE_OFFLOAD:
                for q in range(4):
                    src = (q ^ 1) * 32
                    P.op("act", lambda e, q=q, src=src: e.activation(
                        out=XC[q * 32:(q + 1) * 32, 0:n], in_=PS[src:src + 32, bank, 0:n], func=AF.Copy),
                        [bPS[bank]], [bXC])
                P.op(ROPE_ENG, lambda e: e.tensor_tensor(out=T2[:, 0:n], in0=XC[:, 0:n], in1=Stab, op=ALU.mult),
                     [bXC] + tabbufs, [bT2])
            else:
                for q in range(4):
                    src = (q ^ 1) * 32
                    P.op("dve", lambda e, q=q, src=src: e.tensor_tensor(
                        out=T2[q * 32:(q + 1) * 32, 0:n], in0=PS[src:src + 32, bank, 0:n],
                        in1=Stab[q * 32:(q + 1) * 32, :], op=ALU.mult),
                        [bPS[bank]] + tabbufs, [bT2])
            if not rstd:
                P.op("dve", lambda e: e.tensor_tensor(out=dst, in0=T1[:, 0:n], in1=T2[:, 0:n], op=ALU.add),
                     [bT1, bT2], dstbufs)
            else:
                P.op("dve", lambda e: e.tensor_tensor(out=T1[:, 0:n], in0=T1[:, 0:n], in1=T2[:, 0:n], op=ALU.add),
                     [bT1, bT2], [bT1])
                P.op("dve", lambda e: e.tensor_tensor(out=dst, in0=T1[:, 0:n], in1=RS[:, 0:n], op=ALU.mult),
                     [bT1, bRS], dstbufs)

        def rms(bank, n):
            P.op("act", lambda e: e.activation(out=SQ[:, 0:n], in_=PS[:, bank, 0:n], func=AF.Square),
                 [bPS[bank]], [bSQ])
            b2 = next_bank()
            mm(PS[:, b2, 0:n], BLK1, SQ[:, 0:n], True, True, [bCS, bSQ], [bPS[b2]])
            P.op("act", lambda e: e.activation(out=SD[:, 0:n], in_=PS[:, b2, 0:n], func=AF.Ln,
                                               bias=CS[:, 132:133], scale=1.0 / HD),
                 [bPS[b2], bCS], [bSD])
            P.op("act", lambda e: e.activation(out=RS[:, 0:n], in_=SD[:, 0:n], func=AF.Exp, scale=-0.5), [bSD], [bRS])

        def gain_tabs(n, gcol):
            for j in range(2):
                P.op("pool", lambda e, j=j: e.tensor_scalar(
                    out=GT[:, j, 0:n], in0=TB[:, 2 + j, 0:n], scalar1=CS[:, 128 + gcol + j:128 + gcol + j + 1],
                    scalar2=1.0, op0=ALU.mult, op1=ALU.mult), [bTB, bCS], [bGT])

        SB = [(0, 1), (2, 3)]
        ACCS = [(4, 5), (6, 7)]

        pending = []

        def flush_pending():
            while pending:
                pending.pop(0)()

        def attention(KT, bKT, V, bV, tiles, qoff, sink):
            for c in range(4):
                qc = qoff + c
                nst = len(tiles)
                ACC = ACCS[st["pair"] % 2]
                st["pair"] += 1

                def qk(s):
                    kt, nk, mi, c0, c1 = tiles[s]
                    banks = SB[s % 2]
                    g = min(kt // 4, NG - 1)
                    for h in range(2):
                        b = banks[h]
                        mm(PS[0:nk, b, c0:c1], KT[64 * h:64 * h + 64, kt * 128:kt * 128 + nk],
                           QM[64 * h:64 * h + 64, qc, c0:c1], True, mi is None, [bKT[g], bQM[qc]], [bPS[b]])
                        if mi is not None:
                            mm(PS[0:nk, b, c0:c1], IDN[:, 0:nk], MSK[:, mi, c0:c1], False, True, [bIDN, bMSK], [bPS[b]])

                def ex(s):
                    kt, nk, mi, c0, c1 = tiles[s]
                    banks = SB[s % 2]
                    P.op("act", lambda e: e.activation(
                        out=PT[0:nk, s % 2, :, c0:c1], in_=PS[0:nk, banks[0]:banks[1] + 1, c0:c1], func=AF.Exp,
                        scale=HD ** -0.5), [bPS[banks[0]], bPS[banks[1]]], [bPT[s % 2]])

                def pv(s):
                    kt, nk, mi, c0, c1 = tiles[s]
                    g = min(kt // 4, NG - 1)
                    for h in range(2):
                        mm(PS[0:65, ACC[h], c0:c1], V[0:nk, kt, h, :], PT[0:nk, s % 2, h, c0:c1], s == 0, s == nst - 1,
                           [bV[g], bPT[s % 2]], [bPS[ACC[h]]])

                qk(0)
                ex(0)
                if nst > 1:
                    qk(1)
                    ex(1)
                for s in range(nst):
                    if s + 2 < nst:
                        qk(s + 2)
                    pv(s)
                    if s + 2 < nst:
                        ex(s + 2)
                    if s == min(12, nst - 1):
                        flush_pending()
                for h in range(2):
                    base = 64 * h
                    acc = ACC[h]
                    head = c + 4 * h
                    P.op("dve", lambda e, h=h, base=base, acc=acc, qc=qc: e.tensor_tensor(
                        out=TT[base:base + 64, h, :], in0=PS[0:64, acc, :], in1=SZ[base:base + 64, qc, :], op=ALU.mult),
                        [bPS[acc], bSZ[qc]], [bTT[h]])
                    if sink:
                        P.op("dve", lambda e, h=h, acc=acc, head=head: e.tensor_scalar(
                            out=RR[64:65, h, :], in0=PS[64:65, acc, :], scalar1=ESK[64:65, head:head + 1], scalar2=None,
                            op0=ALU.add), [bPS[acc], bESK], [bRR[h]])
                    else:
                        P.op("dve", lambda e, h=h, acc=acc: e.tensor_copy(out=RR[64:65, h, :], in_=PS[64:65, acc, :]),
                             [bPS[acc]], [bRR[h]])
                for h in range(2):
                    P.op("dve", lambda e, h=h: e.reciprocal(out=RR[64:65, h, :], in_=RR[64:65, h, :]), [bRR[h]], [bRR[h]])
                    P.op("dve", lambda e, h=h: e.tensor_copy(out=RH[64:65, h, :], in_=RR[64:65, h, :]), [bRR[h]], [bRH[h]])
                    P.op("dve", lambda e, h=h: e.tensor_tensor(out=RL[64:65, h, :], in0=RR[64:65, h, :], in1=RH[64:65, h, :],
                                                               op=ALU.subtract), [bRR[h], bRH[h]], [bRL[h]])

                def tail(qc=qc, ACC=ACC):
                    for h in range(2):
                        base = 64 * h
                        bcb = ACC[h]
                        mm(PS[base:base + 64, bcb, :], ONES[64:65, 0:64], RH[64:65, h, :], True, False, [bONES, bRH[h]], [bPS[bcb]])
                        mm(PS[base:base + 64, bcb, :], ONES[64:65, 0:64], RL[64:65, h, :], False, True, [bONES, bRL[h]], [bPS[bcb]])
                        P.op("dve", lambda e, h=h, base=base, qc=qc, bcb=bcb: e.tensor_tensor(
                            out=YT[base:base + 64, qc, :], in0=TT[base:base + 64, h, :], in1=PS[base:base + 64, bcb, :],
                            op=ALU.mult), [bTT[h], bPS[bcb]], [bYT[qc][h]])
                pending.append(tail)

        prologue_slab(S_WK)
        todo_slabs = [s_ for s_ in range(NSLAB) if s_ != S_WK]

        for u in range(NU):
            dma(MSK[:], masks[u].rearrange("m p n -> p m n"), [], [bMSK])
            wk = get_slab(S_WK)
            load_x(u, 0, 512, use_act=True)
            for g in range(NG):
                t0 = g * 512
                n = 512 if g < NG - 1 else N_META
                kxb = st["xb"]
                if g + 1 < NG:
                    load_x(u, t0 + 512, 512 if g + 1 < NG - 1 else N_META, use_act=True)
                load_tabs(u, t0, n)
                b = next_bank()
                proj_fm(wk, 0, n, b, kxb)
                rope(b, n, TB[:, 0, 0:n], TB[:, 1, 0:n], [bTB], KA[:, t0:t0 + n], [bKA[g]])
                b = next_bank()
                proj_fm(wk, 128, n, b, kxb)
                gain_tabs(n, 2)
                rms(b, n)
                rope(b, n, GT[:, 0, 0:n], GT[:, 1, 0:n], [bGT], KB[:, t0:t0 + n], [bKB[g]], rstd=True)
                for tt in range((n + 127) // 128):
                    m = min(128, n - tt * 128)
                    kt = g * 4 + tt
                    b = next_bank()
                    for kc in range(KC):
                        mm(PS[0:m, b, 0:256], XB[:, kxb, kc, tt * 128:tt * 128 + m], WR[:, wk, kc, 256:512], kc == 0, kc == KC - 1,
                           [bWR[wk], bXBk[kxb][kc]], [bPS[b]])
                    P.op("act", lambda e, m=m, kt=kt, b=b: e.activation(
                        out=VA[0:m, kt, :, 0:64], in_=PS[0:m, b, 0:128].rearrange("p (a d) -> p a d", a=2), func=AF.Copy),
                        [bPS[b], bVinit], [bVA[g]])
                    P.op("act", lambda e, m=m, kt=kt, b=b: e.activation(
                        out=VB[0:m, kt, :, 0:64], in_=PS[0:m, b, 128:256].rearrange("p (a d) -> p a d", a=2), func=AF.Copy),
                        [bPS[b], bVinit], [bVB[g]])
                if todo_slabs:
                    prologue_slab(todo_slabs.pop(0), avoid=wk)
            while todo_slabs:
                prologue_slab(todo_slabs.pop(0))

            if dbg and u == 0:
                dma(d_ka[:, :], KA[:], bKA, [])
                dma(d_kb[:, :], KB[:], bKB, [])
                dma(d_va[:, :], VA[:].rearrange("p a b c -> p (a b c)"), bVA, [])
                dma(d_vb[:, :], VB[:].rearrange("p a b c -> p (a b c)"), bVB, [])
            for ci in range(NCH):
                t0 = ci * 512
                if ci == 0:
                    load_x(u, t0, 512)
                    load_tabs(u, t0, 512)
                cur_xb = st["xb"]
                sl = get_slab(S_QA)
                for c in range(4):
                    b = next_bank()
                    proj_fm(sl, c * 128, 512, b)
                    rope(b, 512, TB[:, 0, :], TB[:, 1, :], [bTB], QM[:, c, :], [bQM[c]])
                sl = get_slab(S_QB)
                gain_tabs(512, 0)
                for c in range(4):
                    b = next_bank()
                    proj_fm(sl, c * 128, 512, b)
                    rms(b, 512)
                    rope(b, 512, GT[:, 0, :], GT[:, 1, :], [bGT], QM[:, 4 + c, :], [bQM[4 + c]], rstd=True)
                for off, sid in ((0, S_ZA), (4, S_ZB)):
                    sl = get_slab(sid)
                    for c in range(4):
                        b = next_bank()
                        proj_fm(sl, c * 128, 512, b)
                        P.op("act", lambda e, b=b, c=c, off=off: e.activation(out=SZ[:, off + c, :], in_=PS[:, b, :], func=AF.Silu),
                             [bPS[b]], [bSZ[off + c]])
                if dbg and u == 0 and ci == 0:
                    dma(d_qm[:, :], QM[:].rearrange("p a b -> p (a b)"), bQM, [])
                    dma(d_sz[:, :], SZ[:].rearrange("p a b -> p (a b)"), bSZ, [])
                if ci + 1 < NCH:
                    load_x(u, t0 + 512, 512)
                    load_tabs(u, t0 + 512, 512)
                tiles = [(META_KT, N_META, None, 0, 512)]
                for r in range(-1, 5):
                    t = 4 * ci + r
                    c0, c1 = (128 * max(0, r - 1), 128 * (min(3, r + 1) + 1)) if COLRANGE else (0, 512)
                    if 0 <= t < NBLK:
                        tiles.append((t, 128, r + 1, c0, c1))
                    elif t == -1:
                        tiles.append((2 * NBLK - 1, 128, 6, c0, c1))
                    elif t == NBLK:
                        tiles.append((NBLK, 128, 7, c0, c1))
                attention(KA, bKA, VA, bVA, tiles, 0, True)
                tiles = [(t, 128, None, 0, 512) for t in range(2 * NBLK)] + [(META_KT, N_META, None, 0, 512)]
                attention(KB, bKB, VB, bVB, tiles, 4, False)
                flush_pending()
                if dbg and u == 0 and ci == 0:
                    dma(d_yt[:, :], YT[:].rearrange("p a b -> p (a b)"), [x for p_ in bYT for x in p_], [])
                for hh in range(2):
                    sga = get_slab(S_GA0 + hh)
                    sgb = get_slab(S_GB0 + hh)
                    if hh == 0:
                        swa = get_slab(S_WBA)
                        swb = get_slab(S_WBB)
                        WBAv = WR[:, swa].rearrange("p (k h) c -> p k (h c)", h=2)
                        WBBv = WR[:, swb].rearrange("p (k h) c -> p k (h c)", h=2)
                    for jj in range(4):
                        j = hh * 4 + jj
                        for br, (sg, Wv, sw, yoff) in enumerate(((sga, WBAv, swa, 0), (sgb, WBBv, swb, 4))):
                            b = next_bank()
                            proj_fm(sg, jj * 128, 512, b, cur_xb)
                            P.op("act", lambda e, b=b, br=br: e.activation(out=SG[:, br, :], in_=PS[:, b, :], func=AF.Sigmoid),
                                 [bPS[b]], [bSG[br]])
                            b2 = next_bank()
                            for kc in range(4):
                                mm(PS[:, b2, :], Wv[:, kc, j * 128:(j + 1) * 128], YT[:, yoff + kc, :], kc == 0, kc == 3,
                                   [bWR[sw], bYT[yoff + kc][0], bYT[yoff + kc][1]], [bPS[b2]])
                            tdst, tb = (T1, bT1) if br == 0 else (T2, bT2)
                            P.op("dve", lambda e, b2=b2, br=br, tdst=tdst: e.tensor_tensor(
                                out=tdst[:, :], in0=PS[:, b2, :], in1=SG[:, br, :], op=ALU.mult), [bPS[b2], bSG[br]], [tb])
                        P.op("dve", lambda e, j=j: e.tensor_tensor(out=QM[:, j, :], in0=T1[:, :], in1=T2[:, :], op=ALU.add),
                             [bT1, bT2], [bQM[j]])
                if dbg and u == 0 and ci == 0:
                    dma(d_mg[:, :], QM[:].rearrange("p a b -> p (a b)"), bQM, [])
                wo = [get_slab(S_WO0), get_slab(S_WO1)]
                if ci + 1 < NCH:
                    prefetch_slab(S_QA)
                    prefetch_slab(S_QB)
                elif u + 1 < NU:
                    prefetch_slab(S_WK)
                h2slots = {}

                def h2_load(blk_):
                    hs_ = st["h2"]
                    st["h2"] = (hs_ + 1) % NH2
                    r0_ = t0 + blk_ * 128
                    dma(H2[:, hs_, :], xq[u, r0_:r0_ + 128, :], [], [bH2[hs_], bH2h[hs_][0], bH2h[hs_][1]])
                    h2slots[blk_] = hs_
                for blk in range(NH2 - 1):
                    h2_load(blk)
                for blk in range(4):
                    if blk + NH2 - 1 < 4:
                        h2_load(blk + NH2 - 1)
                    hs = h2slots[blk]
                    r0 = t0 + blk * 128
                    for hc in range(2):
                        b = next_bank()
                        for kc in range(KC):
                            mm(PS[:, b, :], QM[:, kc, blk * 128:(blk + 1) * 128], WR[:, wo[hc], kc, :], kc == 0, kc == KC - 1,
                               [bQM[kc], bWR[wo[hc]]], [bPS[b]])
                        P.op("dve", lambda e, hs=hs, hc=hc, b=b: e.scalar_tensor_tensor(
                            out=H2[:, hs, hc * 512:(hc + 1) * 512], in0=H2[:, hs, hc * 512:(hc + 1) * 512], scalar=ALPHA,
                            in1=PS[:, b, :], op0=ALU.mult, op1=ALU.add), [bH2[hs], bPS[b]], [bH2[hs]])
                        P.op("dve", lambda e, hs=hs, hc=hc: e.bn_stats(out=ST[:, hs, hc * 6:(hc + 1) * 6],
                                                                      in_=H2[:, hs, hc * 512:(hc + 1) * 512]),
                             [bH2[hs]], [bST[hs]])
                    P.op("dve", lambda e, hs=hs: e.bn_aggr(out=MV[:, hs, 0:2], in_=ST[:, hs, :]), [bST[hs]], [bMV[hs]])
                    P.op("act", lambda e, hs=hs: e.activation(out=MV[:, hs, 2:3], in_=MV[:, hs, 1:2], func=AF.Ln,
                                                              bias=CS[:, 133:134], scale=1.0), [bMV[hs], bCS], [bMV[hs]])
                    P.op("act", lambda e, hs=hs: e.activation(out=MV[:, hs, 3:4], in_=MV[:, hs, 2:3], func=AF.Exp, scale=-0.5),
                         [bMV[hs]], [bMV[hs]])
                    P.op("dve", lambda e, hs=hs: e.tensor_scalar(
                        out=H2[:, hs, :], in0=H2[:, hs, :], scalar1=MV[:, hs, 0:1], scalar2=MV[:, hs, 3:4],
                        op0=ALU.subtract, op1=ALU.mult), [bH2[hs], bMV[hs]], [bH2[hs]])
                    for eng_, c0_, c1_ in (("pool", 0, 512), ("dve", 512, 1024)):
                        P.op(eng_, lambda e, hs=hs, c0_=c0_, c1_=c1_: e.tensor_tensor(
                            out=H2[:, hs, c0_:c1_], in0=H2[:, hs, c0_:c1_], in1=LN[:, 0, c0_:c1_], op=ALU.mult),
                            [bH2[hs], bLN], [bH2h[hs][c0_ // 512]])
                        P.op(eng_, lambda e, hs=hs, c0_=c0_, c1_=c1_: e.tensor_tensor(
                            out=H2[:, hs, c0_:c1_], in0=H2[:, hs, c0_:c1_], in1=LN[:, 1, c0_:c1_], op=ALU.add),
                            [bH2h[hs][c0_ // 512], bLN], [bH2h[hs][c0_ // 512]])
                    dma(yout[u, r0:r0 + 128, :], H2[:, hs, :], [bH2[hs], bH2h[hs][0], bH2h[hs][1]], [])

        P.finalize(nc, es)
        block = es.enter_context(nc.Block())

        @block.sync
        def _(e):
            Prog.run(e, P.thunks["sp"])
            for k, v in P.final_dma.items():
                e.wait_ge(P.dsems[k], v)

        @block.tensor
        def _(e):
            Prog.run(e, P.thunks["pe"])

        @block.scalar
        def _(e):
            Prog.run(e, P.thunks["act"])

        @block.vector
        def _(e):
            Prog.run(e, P.thunks["dve"])

        @block.gpsimd
        def _(e):
            Prog.run(e, P.thunks["pool"])

    return nc


PERM_B = np.array(list(range(0, 16)) + list(range(32, 48)) + list(range(16, 32)) + list(range(48, 64)))
OFF = dict(qa=0, ka=512, va=640, za=768, qb=1280, kb=1792, vb=1920, zb=2048, ga=2560, gb=3584)


def _head_cols(off, c, perm):
    return np.concatenate([off + c * 64 + perm, off + (4 + c) * 64 + perm])


def make_slabs(w_in, w_ba, w_bb, w_out):
    nat = np.arange(64)
    cols = {}
    cols[S_QA] = np.concatenate([_head_cols(OFF["qa"], c, nat) for c in range(4)])
    cols[S_QB] = np.concatenate([_head_cols(OFF["qb"], c, PERM_B) for c in range(4)])
    cols[S_ZA] = np.concatenate([_head_cols(OFF["za"], c, nat) for c in range(4)])
    cols[S_ZB] = np.concatenate([_head_cols(OFF["zb"], c, nat) for c in range(4)])
    cols[S_GA0] = OFF["ga"] + np.arange(0, 512)
    cols[S_GA1] = OFF["ga"] + np.arange(512, 1024)
    cols[S_GB0] = OFF["gb"] + np.arange(0, 512)
    cols[S_GB1] = OFF["gb"] + np.arange(512, 1024)
    cols[S_WK] = np.concatenate([OFF["ka"] + np.arange(128),
                                 OFF["kb"] + PERM_B, OFF["kb"] + 64 + PERM_B,
                                 OFF["va"] + np.arange(128), OFF["vb"] + np.arange(128)])
    slabs = np.zeros((NSLAB, 128, KC, 512), np.float32)
    for s, cc in cols.items():
        slabs[s] = w_in[:, cc].reshape(KC, 128, 512).transpose(1, 0, 2)
    rowperm = np.concatenate([np.concatenate([c * 64 + nat, (4 + c) * 64 + nat]) for c in range(4)])
    for s, w in ((S_WBA, w_ba), (S_WBB, w_bb)):
        t = w[rowperm, :].reshape(4, 128, 1024).transpose(1, 0, 2)
        slabs[s] = t.reshape(128, 4, 2, 512).reshape(128, 8, 512)
    for hc, s in ((0, S_WO0), (1, S_WO1)):
        slabs[s] = w_out[:, hc * 512:(hc + 1) * 512].reshape(KC, 128, 512).transpose(1, 0, 2)
    return slabs


def make_tables(NT, S, hf):
    own = np.arange(hf * NT, (hf + 1) * NT)
    oth = np.arange((1 - hf) * NT, (2 - hf) * NT)
    real = np.concatenate([own, oth])
    metai = np.arange(N_META)
    posA = np.concatenate([N_META + real, metai]).astype(np.float32)
    rowB = np.concatenate([real // GRID_W, metai - N_META]).astype(np.float32)
    colB = np.concatenate([real % GRID_W, metai - N_META]).astype(np.float32)
    LK = 2 * NT + N_META
    inv32 = (ROPE_THETA ** (-np.arange(0, 64, 2, dtype=np.float32) / 64)).astype(np.float32)
    inv16 = (ROPE_THETA ** (-np.arange(0, 32, 2, dtype=np.float32) / 32)).astype(np.float32)
    tab = np.zeros((4, 128, LK), np.float32)
    for p in range(128):
        d = p % 64
        angA = posA * inv32[d % 32]
        tab[0, p] = np.cos(angA)
        tab[1, p] = np.sin(angA) * (-1.0 if d < 32 else 1.0)
        od = PERM_B[d]
        pos = rowB if od < 32 else colB
        angB = pos * inv16[od % 16]
        tab[2, p] = np.cos(angB)
        tab[3, p] = np.sin(angB) * (-1.0 if (od % 32) < 16 else 1.0)
    return tab


def make_masks(hf):
    m = np.zeros((8, 128, 512), np.float32)
    j = np.arange(128)[:, None]
    i = np.arange(128)[None, :]
    for r in range(-1, 5):
        for qb in range(4):
            rel = (r - qb) * 128 + j - i
            m[r + 1, :, qb * 128:(qb + 1) * 128] = np.where(np.abs(rel) <= 128, 0.0, NEG)
    m[6] = m[0] if hf == 1 else NEG
    m[7] = m[5] if hf == 0 else NEG
    return m.astype(ml_dtypes.bfloat16)


def make_consts(q_norm_b, k_norm_b):
    cs = np.zeros((128, 134), np.float32)
    cs[:, 132] = RMS_EPS
    cs[:, 133] = LN_EPS
    p = np.arange(128)
    cs[:, 0:128] = (p[:, None] // 64 == p[None, :] // 64).astype(np.float32)
    d = p % 64
    sw = np.where(d < 32, d + 32, d - 32)
    cs[:, 128] = q_norm_b[PERM_B[d]]
    cs[:, 129] = q_norm_b[PERM_B[sw]]
    cs[:, 130] = k_norm_b[PERM_B[d]]
    cs[:, 131] = k_norm_b[PERM_B[sw]]
    return cs


def unit_inputs(x_seq, meta_tokens, hf, NT):
    own = x_seq[hf * NT:(hf + 1) * NT]
    oth = x_seq[(1 - hf) * NT:(2 - hf) * NT]
    xT = np.ascontiguousarray(np.concatenate([own, oth, meta_tokens], axis=0).T)
    return xT, np.ascontiguousarray(own)


_NC_CACHE = {}


def kernel(x_prompt, x_sample, meta_tokens, w_in, attn_a_sink, q_norm_b, k_norm_b,
           w_branch_a, w_branch_b, w_out, ln_gain, ln_bias):
    f = lambda a: np.asarray(a, dtype=np.float32)
    x_prompt, x_sample, meta_tokens = f(x_prompt), f(x_sample), f(meta_tokens)
    S = x_prompt.shape[1]
    NT = S // 2
    seqs = [x_prompt[b] for b in range(x_prompt.shape[0])] + [x_sample[b] for b in range(x_sample.shape[0])]
    n_units = 2 * len(seqs)
    NU = n_units // N_CORES
    assert NU * N_CORES == n_units
    slabs = make_slabs(f(w_in)[0], f(w_branch_a)[0], f(w_branch_b)[0], f(w_out)[0])
    cs = make_consts(f(q_norm_b)[0], f(k_norm_b)[0])
    ident = np.eye(128, dtype=np.float32).astype(ml_dtypes.bfloat16)
    lngb = np.stack([np.broadcast_to(f(ln_gain)[0][None, :], (128, D_MODEL)),
                     np.broadcast_to(f(ln_bias)[0][None, :], (128, D_MODEL))]).astype(np.float32)
    sink = f(attn_a_sink)[0][None, :]
    tabs_hf = [make_tables(NT, S, 0), make_tables(NT, S, 1)]
    masks_hf = [make_masks(0), make_masks(1)]
    in_maps = []
    for c in range(N_CORES):
        xTs, xqs, tbs, mks = [], [], [], []
        for j in range(NU):
            uid = c * NU + j
            s, hf = uid // 2, uid % 2
            xT_, xq_ = unit_inputs(seqs[s], meta_tokens, hf, NT)
            xTs.append(xT_)
            xqs.append(xq_)
            tbs.append(tabs_hf[hf])
            mks.append(masks_hf[hf])
        in_maps.append(dict(xT=np.stack(xTs), xq=np.stack(xqs), tabs=np.stack(tbs), masks=np.stack(mks),
                            wsrc=slabs, cst=cs, ident=ident, lngb=lngb, sink=sink))
    key = (NT, NU)
    if key not in _NC_CACHE:
        _NC_CACHE[key] = build_program(NT, NU)
    nc = _NC_CACHE[key]
    res = run_bass_kernel_spmd(nc, in_maps, core_ids=list(range(N_CORES)))
    outs = [np.zeros((S, D_MODEL), np.float32) for _ in seqs]
    for c in range(N_CORES):
        yc = np.asarray(res.results[c]["y"])
        for j in range(NU):
            uid = c * NU + j
            s, hf = uid // 2, uid % 2
            outs[s][hf * NT:(hf + 1) * NT] = yc[j]
    nb = x_prompt.shape[0]
    y_prompt = np.stack(outs[:nb]).astype(np.float32)
    y_sample = np.stack(outs[nb:]).astype(np.float32)
    return (y_prompt, y_sample)
```

```python
import numpy as np
import ml_dtypes
from contextlib import ExitStack

import concourse.bass as bass
import concourse.mybir as mybir
from concourse.bass_utils import run_bass_kernel_spmd

F32 = mybir.dt.float32
BF16 = mybir.dt.bfloat16
AF = mybir.ActivationFunctionType
ALU = mybir.AluOpType

D_MODEL = 1024
KC = 8
HD = 64
N_META = 16
A_HEADS = 8
GRID_W = 64
ROPE_THETA = 10000.0
LN_EPS = 1e-5
RMS_EPS = 1e-6
ALPHA = 2.0 ** 0.25
NEG = -30000.0
N_CORES = 8
UNITS_PER_CORE = 3
NSLAB = 13
ROPE_OFFLOAD = False
ROPE_ENG = "pool"
COLRANGE = True
(S_QA, S_QB, S_ZA, S_ZB, S_GA0, S_GA1, S_GB0, S_GB1, S_WBA, S_WBB, S_WO0, S_WO1, S_WK) = range(NSLAB)

COMPUTE = ("pe", "act", "dve", "pool")
N_DMA_SEMS = 24


class Buf:
    __slots__ = ("name", "w", "r", "rd")

    def __init__(self, name):
        self.name = name
        self.w = None
        self.r = {}
        self.rd = []


class Prog:
    def __init__(self):
        self.ins = []
        self.order = {e: [] for e in COMPUTE + ("sp",)}

    def op(self, eng, fn, reads=(), writes=(), dma=False):
        i = len(self.ins)
        deps = set()
        for b in reads:
            if b.w is not None:
                deps.add(b.w)
        for b in writes:
            if b.w is not None:
                deps.add(b.w)
            deps.update(b.r.values())
            deps.update(b.rd)
        self.ins.append(dict(eng=eng, fn=fn, deps=deps, dma=dma))
        self.order[eng].append(i)
        for b in reads:
            if dma:
                b.rd.append(i)
            else:
                b.r[eng] = i
        for b in writes:
            b.w = i
            b.r = {}
            b.rd = []
        return i

    def finalize(self, nc, es):
        ins = self.ins
        dma_ids = [i for i, x in enumerate(ins) if x["dma"]]
        for k, i in enumerate(dma_ids):
            if k >= N_DMA_SEMS:
                ins[i]["deps"].add(dma_ids[k - N_DMA_SEMS])
        needed = set()
        for i, x in enumerate(ins):
            for d in x["deps"]:
                if ins[d]["eng"] == "pe" and x["eng"] == "pe" and not ins[d]["dma"] and not x["dma"]:
                    continue
                needed.add(d)
        sems = {e: es.enter_context(nc.semaphore("s_" + e)) for e in COMPUTE}
        dsems = [es.enter_context(nc.semaphore("s_dma%d" % k)) for k in range(N_DMA_SEMS)]
        sig = {}
        cnt = {e: 0 for e in COMPUTE}
        for i, x in enumerate(ins):
            if x["dma"]:
                continue
            if i in needed:
                cnt[x["eng"]] += 1
                sig[i] = (sems[x["eng"]], cnt[x["eng"]], 1)
        for k, i in enumerate(dma_ids):
            sig[i] = (dsems[k % N_DMA_SEMS], 16 * (k // N_DMA_SEMS + 1), 16)
        self.final_dma = {}
        for k, i in enumerate(dma_ids):
            self.final_dma[k % N_DMA_SEMS] = 16 * (k // N_DMA_SEMS + 1)
        thunks = {e: [] for e in self.order}
        for e, lst in self.order.items():
            waited = {}
            for i in lst:
                x = ins[i]
                for d in sorted(x["deps"]):
                    if d not in sig:
                        continue
                    sem, val, _ = sig[d]
                    key = id(sem)
                    if waited.get(key, 0) >= val:
                        continue
                    waited[key] = val
                    thunks[e].append(("w", sem, val))
                thunks[e].append(("i", x["fn"], sig.get(i)))
        self.thunks = thunks
        self.dsems = dsems

    @staticmethod
    def run(eng, lst):
        for t in lst:
            if t[0] == "w":
                eng.wait_ge(t[1], t[2])
            else:
                inst = t[1](eng)
                if t[2] is not None:
                    inst.then_inc(t[2][0], t[2][2])


def build_program(NT, NU, dbg=False):
    NBLK = NT // 128
    NCH = NT // 512
    LK = 2 * NT + N_META
    NKT = 2 * NBLK + 1
    NG = 2 * NCH + 1
    META_KT = 2 * NBLK

    nc = bass.Bass("TRN2", target_bir_lowering=False)
    P = Prog()

    def dram(name, shape, dt, kind):
        return nc.dram_tensor(name, shape, dt, kind=kind).ap()

    xT = dram("xT", [NU, D_MODEL, LK], F32, "ExternalInput")
    xq = dram("xq", [NU, NT, D_MODEL], F32, "ExternalInput")
    tabs = dram("tabs", [NU, 4, 128, LK], F32, "ExternalInput")
    masks = dram("masks", [NU, 8, 128, 512], BF16, "ExternalInput")
    wsrc = dram("wsrc", [NSLAB, 128, KC, 512], F32, "ExternalInput")
    cst = dram("cst", [128, 128 + 6], F32, "ExternalInput")
    identd = dram("ident", [128, 128], BF16, "ExternalInput")
    lngb = dram("lngb", [2, 128, D_MODEL], F32, "ExternalInput")
    sinkd = dram("sink", [1, A_HEADS], F32, "ExternalInput")
    yout = dram("y", [NU, NT, D_MODEL], F32, "ExternalOutput")
    wsc = dram("wsc", [NSLAB, 128, KC, 512], BF16, "Internal")
    if dbg:
        d_ka = dram("d_ka", [128, LK], BF16, "ExternalOutput")
        d_kb = dram("d_kb", [128, LK], BF16, "ExternalOutput")
        d_va = dram("d_va", [128, NKT * 130], BF16, "ExternalOutput")
        d_vb = dram("d_vb", [128, NKT * 130], BF16, "ExternalOutput")
        d_qm = dram("d_qm", [128, 8 * 512], BF16, "ExternalOutput")
        d_sz = dram("d_sz", [128, 8 * 512], BF16, "ExternalOutput")
        d_yt = dram("d_yt", [128, 8 * 512], BF16, "ExternalOutput")
        d_mg = dram("d_mg", [128, 8 * 512], BF16, "ExternalOutput")

    with ExitStack() as es:
        def sb(name, shape, dt):
            return es.enter_context(nc.sbuf_tensor(name, shape, dt))

        KA = sb("KA", [128, LK], BF16)
        KB = sb("KB", [128, LK], BF16)
        VA = sb("VA", [128, NKT, 2, 65], BF16)
        VB = sb("VB", [128, NKT, 2, 65], BF16)
        NW = 4
        WR = sb("WR", [128, NW, KC, 512], BF16)
        NXS = 4
        XS = sb("XS", [128, NXS, 512], F32)
        XB = sb("XB", [128, 2, KC, 512], BF16)
        XC = sb("XC", [128, 512], F32) if ROPE_OFFLOAD else None
        TB = sb("TB", [128, 4, 512], F32)
        GT = sb("GT", [128, 2, 512], F32)
        MSK = sb("MSK", [128, 8, 512], BF16)
        QM = sb("QM", [128, 8, 512], BF16)
        SZ = sb("SZ", [128, 8, 512], BF16)
        PT = sb("PT", [128, 2, 2, 512], BF16)
        YT = sb("YT", [128, 8, 512], BF16)
        RS = sb("RS", [128, 512], F32)
        TT = sb("TT", [128, 2, 512], F32)
        RR = sb("RR", [128, 2, 512], F32)
        T1, T2 = TT[:, 0, :], TT[:, 1, :]
        SQ, SD = RR[:, 0, :], RR[:, 1, :]
        RH = sb("RH", [128, 2, 512], BF16)
        RL = sb("RL", [128, 2, 512], BF16)
        SG = sb("SG", [128, 2, 512], BF16)
        NH2 = 3
        H2 = sb("H2", [128, NH2, D_MODEL], F32)
        LN = sb("LN", [128, 2, D_MODEL], F32)
        ST = sb("ST", [128, 3, 12], F32)
        MV = sb("MV", [128, 3, 4], F32)
        CS = sb("CS", [128, 128 + 6], F32)
        IDN = sb("IDN", [128, 128], BF16)
        ONES = sb("ONES", [128, 64], BF16)
        ESK = sb("ESK", [128, A_HEADS], F32)
        PS = es.enter_context(nc.psum_tensor("PS", [128, 8, 512], F32))

        bKA = [Buf("KA%d" % g) for g in range(NG)]
        bKB = [Buf("KB%d" % g) for g in range(NG)]
        bVA = [Buf("VA%d" % g) for g in range(NG)]
        bVB = [Buf("VB%d" % g) for g in range(NG)]
        bWR = [Buf("WR%d" % i) for i in range(NW)]
        bXS = [Buf("XS%d" % i) for i in range(NXS)]
        bXBk = [[Buf("XB%d_%d" % (j, k)) for k in range(KC)] for j in range(2)]
        bXC = Buf("XC")
        bTB = Buf("TB")
        bGT = Buf("GT")
        bMSK = Buf("MSK")
        bQM = [Buf("QM%d" % i) for i in range(8)]
        bSZ = [Buf("SZ%d" % i) for i in range(8)]
        bPT = [Buf("PT%d" % i) for i in range(2)]
        bYT = [[Buf("YT%d_%d" % (i, h)) for h in range(2)] for i in range(8)]
        bRS = Buf("RS")
        bTT = [Buf("TT0"), Buf("TT1")]
        bRR = [Buf("RR0"), Buf("RR1")]
        bT1, bT2 = bTT
        bSQ, bSD = bRR
        bRH = [Buf("RH0"), Buf("RH1")]
        bRL = [Buf("RL0"), Buf("RL1")]
        bSG = [Buf("SG0"), Buf("SG1")]
        bH2 = [Buf("H2%d" % i) for i in range(3)]
        bST = [Buf("ST%d" % i) for i in range(3)]
        bMV = [Buf("MV%d" % i) for i in range(3)]
        bH2h = [[Buf("H2h%d_%d" % (i, j)) for j in range(2)] for i in range(3)]
        bLN, bCS, bIDN, bONES, bESK = Buf("LN"), Buf("CS"), Buf("IDN"), Buf("ONES"), Buf("ESK")
        bPS = [Buf("PS%d" % i) for i in range(8)]
        bWSC = [Buf("WSC%d" % i) for i in range(NSLAB)]
        bVinit = Buf("Vinit")

        st = dict(xs=0, bank=0, wslot=0, xb=0, h2=0, pair=0)

        def next_bank():
            b = st["bank"]
            st["bank"] = (b + 1) % 8
            return b

        def dma(out, in_, reads, writes):
            return P.op("sp", lambda e, o=out, i=in_: e.dma_start(out=o, in_=i), reads, writes, dma=True)

        dma(CS[:], cst[:, :], [], [bCS])
        dma(IDN[:], identd[:, :], [], [bIDN])
        dma(LN[:], lngb.rearrange("a p n -> p a n"), [], [bLN])
        dma(ESK[64:65, :], sinkd[:, :], [], [bESK])
        P.op("act", lambda e: e.activation(out=ESK[64:65, :], in_=ESK[64:65, :], func=AF.Exp), [bESK], [bESK])
        P.op("pool", lambda e: e.memset(ONES[:], 1.0), [], [bONES])
        P.op("pool", lambda e: e.memset(VA[:].rearrange("p a b c -> p (a b c)"), 1.0), [], [bVinit] + bVA)
        P.op("pool", lambda e: e.memset(VB[:].rearrange("p a b c -> p (a b c)"), 1.0), [], [bVinit] + bVB)
        BLK1 = CS[:, 0:128]

        def load_x(u, t0, n, use_act=False):
            st["xb"] ^= 1
            xb = st["xb"]
            for kc in range(KC):
                s = st["xs"]
                st["xs"] = (s + 1) % NXS
                dma(XS[:, s, 0:n], xT[u, kc * 128:(kc + 1) * 128, t0:t0 + n], [], [bXS[s]])
                if use_act and kc % 2 == 1:
                    P.op("act", lambda e, s=s, kc=kc, xb=xb: e.activation(out=XB[:, xb, kc, 0:n], in_=XS[:, s, 0:n], func=AF.Copy),
                         [bXS[s]], [bXBk[xb][kc]])
                else:
                    P.op("pool", lambda e, s=s, kc=kc, xb=xb: e.tensor_copy(out=XB[:, xb, kc, 0:n], in_=XS[:, s, 0:n]),
                         [bXS[s]], [bXBk[xb][kc]])

        def load_tabs(u, t0, n):
            dma(TB[:, :, 0:n], tabs[u, :, :, t0:t0 + n].rearrange("f p n -> p f n"), [], [bTB])

        def prologue_slab(s, avoid=None):
            slot = st["wslot"]
            if slot == avoid:
                slot = (slot + 1) % NW
            st["wslot"] = (slot + 1) % NW
            for kc in range(KC):
                xs = st["xs"]
                st["xs"] = (xs + 1) % NXS
                dma(XS[:, xs, :], wsrc[s, :, kc, :], [], [bXS[xs]])
                if kc % 3 != 2:
                    P.op("act", lambda e, xs=xs, kc=kc, slot=slot: e.activation(out=WR[:, slot, kc, :], in_=XS[:, xs, :], func=AF.Copy),
                         [bXS[xs]], [bWR[slot]])
                else:
                    P.op("pool", lambda e, xs=xs, kc=kc, slot=slot: e.tensor_copy(out=WR[:, slot, kc, :], in_=XS[:, xs, :]),
                         [bXS[xs]], [bWR[slot]])
            dma(wsc[s], WR[:, slot], [bWR[slot]], [bWSC[s]])

        pref = {}

        def prefetch_slab(s):
            slot = st["wslot"]
            st["wslot"] = (slot + 1) % NW
            dma(WR[:, slot], wsc[s], [bWSC[s]], [bWR[slot]])
            pref.setdefault(s, []).append(slot)

        def get_slab(s):
            if not pref.get(s):
                prefetch_slab(s)
            return pref[s].pop(0)

        def mm(out, lhsT, rhs, start, stop, reads, writes):
            return P.op("pe", lambda e: e.matmul(out, lhsT=lhsT, rhs=rhs, start=start, stop=stop), reads, writes)

        def proj_fm(slot, c0, n, bank, xb=None):
            if xb is None:
                xb = st["xb"]
            for kc in range(KC):
                mm(PS[:, bank, 0:n], WR[:, slot, kc, c0:c0 + 128], XB[:, xb, kc, 0:n], kc == 0, kc == KC - 1,
                   [bWR[slot], bXBk[xb][kc]], [bPS[bank]])

        def rope(bank, n, Ctab, Stab, tabbufs, dst, dstbufs, rstd=False):
            ps = PS[:, bank, 0:n]
            P.op("dve", lambda e: e.tensor_tensor(out=T1[:, 0:n], in0=ps, in1=Ctab, op=ALU.mult),
                 [bPS[bank]] + tabbufs, [bT1])
            if ROPE_OFFLOAD:
                for q in range(4):
                    src = (q ^ 1) * 32
                    P.op("act", lambda e, q=q, src=src: e.activation(
                        out=XC[q * 32:(q + 1) * 32, 0:n], in_=PS[src:src + 32, bank, 0:n], func=AF.Copy),
                        [bPS[bank]], [bXC])
                P.op(ROPE_ENG, lambda e: e.tensor_tensor(out=T2[:, 0:n], in0=XC[:, 0:n], in1=Stab, op=ALU.mult),
                     [bXC] + tabbufs, [bT2])
            else:
                for q in range(4):
                    src = (q ^ 1) * 32
                    P.op("dve", lambda e, q=q, src=src: e.tensor_tensor(
                        out=T2[q * 32:(q + 1) * 32, 0:n], in0=PS[src:src + 32, bank, 0:n],
                        in1=Stab[q * 32:(q + 1) * 32, :], op=ALU.mult),
                        [bPS[bank]] + tabbufs, [bT2])
            if not rstd:
                P.op("dve", lambda e: e.tensor_tensor(out=dst, in0=T1[:, 0:n], in1=T2[:, 0:n], op=ALU.add),
                     [bT1, bT2], dstbufs)
            else:
                P.op("dve", lambda e: e.tensor_tensor(out=T1[:, 0:n], in0=T1[:, 0:n], in1=T2[:, 0:n], op=ALU.add),
                     [bT1, bT2], [bT1])
                P.op("dve", lambda e: e.tensor_tensor(out=dst, in0=T1[:, 0:n], in1=RS[:, 0:n], op=ALU.mult),
                     [bT1, bRS], dstbufs)

        def rms(bank, n):
            P.op("act", lambda e: e.activation(out=SQ[:, 0:n], in_=PS[:, bank, 0:n], func=AF.Square),
                 [bPS[bank]], [bSQ])
            b2 = next_bank()
            mm(PS[:, b2, 0:n], BLK1, SQ[:, 0:n], True, True, [bCS, bSQ], [bPS[b2]])
            P.op("act", lambda e: e.activation(out=SD[:, 0:n], in_=PS[:, b2, 0:n], func=AF.Ln,
                                               bias=CS[:, 132:133], scale=1.0 / HD),
                 [bPS[b2], bCS], [bSD])
            P.op("act", lambda e: e.activation(out=RS[:, 0:n], in_=SD[:, 0:n], func=AF.Exp, scale=-0.5), [bSD], [bRS])

        def gain_tabs(n, gcol):
            for j in range(2):
                P.op("pool", lambda e, j=j: e.tensor_scalar(
                    out=GT[:, j, 0:n], in0=TB[:, 2 + j, 0:n], scalar1=CS[:, 128 + gcol + j:128 + gcol + j + 1],
                    scalar2=1.0, op0=ALU.mult, op1=ALU.mult), [bTB, bCS], [bGT])

        SB = [(0, 1), (2, 3)]
        ACCS = [(4, 5), (6, 7)]

        pending = []

        def flush_pending():
            while pending:
                pending.pop(0)()

        def attention(KT, bKT, V, bV, tiles, qoff, sink):
            for c in range(4):
                qc = qoff + c
                nst = len(tiles)
                ACC = ACCS[st["pair"] % 2]
                st["pair"] += 1

                def qk(s):
                    kt, nk, mi, c0, c1 = tiles[s]
                    banks = SB[s % 2]
                    g = min(kt // 4, NG - 1)
                    for h in range(2):
                        b = banks[h]
                        mm(PS[0:nk, b, c0:c1], KT[64 * h:64 * h + 64, kt * 128:kt * 128 + nk],
                           QM[64 * h:64 * h + 64, qc, c0:c1], True, mi is None, [bKT[g], bQM[qc]], [bPS[b]])
                    if mi is not None:
                        for h in range(2):
                            b = banks[h]
                            mm(PS[0:nk, b, c0:c1], IDN[:, 0:nk], MSK[:, mi, c0:c1], False, True, [bIDN, bMSK], [bPS[b]])

                def ex(s):
                    kt, nk, mi, c0, c1 = tiles[s]
                    banks = SB[s % 2]
                    P.op("act", lambda e: e.activation(
                        out=PT[0:nk, s % 2, :, c0:c1], in_=PS[0:nk, banks[0]:banks[1] + 1, c0:c1], func=AF.Exp,
                        scale=HD ** -0.5), [bPS[banks[0]], bPS[banks[1]]], [bPT[s % 2]])

                def pv(s):
                    kt, nk, mi, c0, c1 = tiles[s]
                    g = min(kt // 4, NG - 1)
                    for h in range(2):
                        mm(PS[0:65, ACC[h], c0:c1], V[0:nk, kt, h, :], PT[0:nk, s % 2, h, c0:c1], s == 0, s == nst - 1,
                           [bV[g], bPT[s % 2]], [bPS[ACC[h]]])

                qk(0)
                ex(0)
                if nst > 1:
                    qk(1)
                    ex(1)
                for s in range(nst):
                    if s + 2 < nst:
                        qk(s + 2)
                    pv(s)
                    if s + 2 < nst:
                        ex(s + 2)
                    if s == min(12, nst - 1):
                        flush_pending()
                for h in range(2):
                    base = 64 * h
                    acc = ACC[h]
                    head = c + 4 * h
                    P.op("dve", lambda e, h=h, base=base, acc=acc, qc=qc: e.tensor_tensor(
                        out=TT[base:base + 64, h, :], in0=PS[0:64, acc, :], in1=SZ[base:base + 64, qc, :], op=ALU.mult),
                        [bPS[acc], bSZ[qc]], [bTT[h]])
                    if sink:
                        P.op("dve", lambda e, h=h, acc=acc, head=head: e.tensor_scalar(
                            out=RR[64:65, h, :], in0=PS[64:65, acc, :], scalar1=ESK[64:65, head:head + 1], scalar2=None,
                            op0=ALU.add), [bPS[acc], bESK], [bRR[h]])
                    else:
                        P.op("dve", lambda e, h=h, acc=acc: e.tensor_copy(out=RR[64:65, h, :], in_=PS[64:65, acc, :]),
                             [bPS[acc]], [bRR[h]])
                for h in range(2):
                    P.op("dve", lambda e, h=h: e.reciprocal(out=RR[64:65, h, :], in_=RR[64:65, h, :]), [bRR[h]], [bRR[h]])
                    P.op("dve", lambda e, h=h: e.tensor_copy(out=RH[64:65, h, :], in_=RR[64:65, h, :]), [bRR[h]], [bRH[h]])
                    P.op("dve", lambda e, h=h: e.tensor_tensor(out=RL[64:65, h, :], in0=RR[64:65, h, :], in1=RH[64:65, h, :],
                                                               op=ALU.subtract), [bRR[h], bRH[h]], [bRL[h]])

                def tail(qc=qc, ACC=ACC):
                    for h in range(2):
                        base = 64 * h
                        bcb = ACC[h]
                        mm(PS[base:base + 64, bcb, :], ONES[64:65, 0:64], RH[64:65, h, :], True, False, [bONES, bRH[h]], [bPS[bcb]])
                        mm(PS[base:base + 64, bcb, :], ONES[64:65, 0:64], RL[64:65, h, :], False, True, [bONES, bRL[h]], [bPS[bcb]])
                        P.op("dve", lambda e, h=h, base=base, qc=qc, bcb=bcb: e.tensor_tensor(
                            out=YT[base:base + 64, qc, :], in0=TT[base:base + 64, h, :], in1=PS[base:base + 64, bcb, :],
                            op=ALU.mult), [bTT[h], bPS[bcb]], [bYT[qc][h]])
                pending.append(tail)

        prologue_slab(S_WK)
        todo_slabs = [s_ for s_ in range(NSLAB) if s_ != S_WK]

        for u in range(NU):
            dma(MSK[:], masks[u].rearrange("m p n -> p m n"), [], [bMSK])
            wk = get_slab(S_WK)
            load_x(u, 0, 512, use_act=True)
            for g in range(NG):
                t0 = g * 512
                n = 512 if g < NG - 1 else N_META
                kxb = st["xb"]
                if g + 1 < NG:
                    load_x(u, t0 + 512, 512 if g + 1 < NG - 1 else N_META, use_act=True)
                load_tabs(u, t0, n)
                b = next_bank()
                proj_fm(wk, 0, n, b, kxb)
                rope(b, n, TB[:, 0, 0:n], TB[:, 1, 0:n], [bTB], KA[:, t0:t0 + n], [bKA[g]])
                b = next_bank()
                proj_fm(wk, 128, n, b, kxb)
                gain_tabs(n, 2)
                rms(b, n)
                rope(b, n, GT[:, 0, 0:n], GT[:, 1, 0:n], [bGT], KB[:, t0:t0 + n], [bKB[g]], rstd=True)
                for tt in range((n + 127) // 128):
                    m = min(128, n - tt * 128)
                    kt = g * 4 + tt
                    b = next_bank()
                    for kc in range(KC):
                        mm(PS[0:m, b, 0:256], XB[:, kxb, kc, tt * 128:tt * 128 + m], WR[:, wk, kc, 256:512], kc == 0, kc == KC - 1,
                           [bWR[wk], bXBk[kxb][kc]], [bPS[b]])
                    P.op("act", lambda e, m=m, kt=kt, b=b: e.activation(
                        out=VA[0:m, kt, :, 0:64], in_=PS[0:m, b, 0:128].rearrange("p (a d) -> p a d", a=2), func=AF.Copy),
                        [bPS[b], bVinit], [bVA[g]])
                    P.op("act", lambda e, m=m, kt=kt, b=b: e.activation(
                        out=VB[0:m, kt, :, 0:64], in_=PS[0:m, b, 128:256].rearrange("p (a d) -> p a d", a=2), func=AF.Copy),
                        [bPS[b], bVinit], [bVB[g]])
                if todo_slabs:
                    prologue_slab(todo_slabs.pop(0), avoid=wk)
            while todo_slabs:
                prologue_slab(todo_slabs.pop(0))

            if dbg and u == 0:
                dma(d_ka[:, :], KA[:], bKA, [])
                dma(d_kb[:, :], KB[:], bKB, [])
                dma(d_va[:, :], VA[:].rearrange("p a b c -> p (a b c)"), bVA, [])
                dma(d_vb[:, :], VB[:].rearrange("p a b c -> p (a b c)"), bVB, [])
            for ci in range(NCH):
                t0 = ci * 512
                if ci == 0:
                    load_x(u, t0, 512)
                    load_tabs(u, t0, 512)
                cur_xb = st["xb"]
                sl = get_slab(S_QA)
                for c in range(4):
                    b = next_bank()
                    proj_fm(sl, c * 128, 512, b)
                    rope(b, 512, TB[:, 0, :], TB[:, 1, :], [bTB], QM[:, c, :], [bQM[c]])
                sl = get_slab(S_QB)
                gain_tabs(512, 0)
                for c in range(4):
                    b = next_bank()
                    proj_fm(sl, c * 128, 512, b)
                    rms(b, 512)
                    rope(b, 512, GT[:, 0, :], GT[:, 1, :], [bGT], QM[:, 4 + c, :], [bQM[4 + c]], rstd=True)
                for off, sid in ((0, S_ZA), (4, S_ZB)):
                    sl = get_slab(sid)
                    for c in range(4):
                        b = next_bank()
                        proj_fm(sl, c * 128, 512, b)
                        P.op("act", lambda e, b=b, c=c, off=off: e.activation(out=SZ[:, off + c, :], in_=PS[:, b, :], func=AF.Silu),
                             [bPS[b]], [bSZ[off + c]])
                if dbg and u == 0 and ci == 0:
                    dma(d_qm[:, :], QM[:].rearrange("p a b -> p (a b)"), bQM, [])
                    dma(d_sz[:, :], SZ[:].rearrange("p a b -> p (a b)"), bSZ, [])
                if ci + 1 < NCH:
                    load_x(u, t0 + 512, 512)
                    load_tabs(u, t0 + 512, 512)
                tiles = [(META_KT, N_META, None, 0, 512)]
                for r in range(-1, 5):
                    t = 4 * ci + r
                    c0, c1 = (128 * max(0, r - 1), 128 * (min(3, r + 1) + 1)) if COLRANGE else (0, 512)
                    if 0 <= t < NBLK:
                        tiles.append((t, 128, r + 1, c0, c1))
                    elif t == -1:
                        tiles.append((2 * NBLK - 1, 128, 6, c0, c1))
                    elif t == NBLK:
                        tiles.append((NBLK, 128, 7, c0, c1))
                attention(KA, bKA, VA, bVA, tiles, 0, True)
                tiles = [(t, 128, None, 0, 512) for t in range(2 * NBLK)] + [(META_KT, N_META, None, 0, 512)]
                attention(KB, bKB, VB, bVB, tiles, 4, False)
                flush_pending()
                if dbg and u == 0 and ci == 0:
                    dma(d_yt[:, :], YT[:].rearrange("p a b -> p (a b)"), [x for p_ in bYT for x in p_], [])
                for hh in range(2):
                    sga = get_slab(S_GA0 + hh)
                    sgb = get_slab(S_GB0 + hh)
                    if hh == 0:
                        swa = get_slab(S_WBA)
                        swb = get_slab(S_WBB)
                        WBAv = WR[:, swa].rearrange("p (k h) c -> p k (h c)", h=2)
                        WBBv = WR[:, swb].rearrange("p (k h) c -> p k (h c)", h=2)
                    for jj in range(4):
                        j = hh * 4 + jj
                        for br, (sg, Wv, sw, yoff) in enumerate(((sga, WBAv, swa, 0), (sgb, WBBv, swb, 4))):
                            b = next_bank()
                            proj_fm(sg, jj * 128, 512, b, cur_xb)
                            P.op("act", lambda e, b=b, br=br: e.activation(out=SG[:, br, :], in_=PS[:, b, :], func=AF.Sigmoid),
                                 [bPS[b]], [bSG[br]])
                            b2 = next_bank()
                            for kc in range(4):
                                mm(PS[:, b2, :], Wv[:, kc, j * 128:(j + 1) * 128], YT[:, yoff + kc, :], kc == 0, kc == 3,
                                   [bWR[sw], bYT[yoff + kc][0], bYT[yoff + kc][1]], [bPS[b2]])
                            tdst, tb = (T1, bT1) if br == 0 else (T2, bT2)
                            P.op("dve", lambda e, b2=b2, br=br, tdst=tdst: e.tensor_tensor(
                                out=tdst[:, :], in0=PS[:, b2, :], in1=SG[:, br, :], op=ALU.mult), [bPS[b2], bSG[br]], [tb])
                        P.op("dve", lambda e, j=j: e.tensor_tensor(out=QM[:, j, :], in0=T1[:, :], in1=T2[:, :], op=ALU.add),
                             [bT1, bT2], [bQM[j]])
                if dbg and u == 0 and ci == 0:
                    dma(d_mg[:, :], QM[:].rearrange("p a b -> p (a b)"), bQM, [])
                wo = [get_slab(S_WO0), get_slab(S_WO1)]
                if ci + 1 < NCH:
                    prefetch_slab(S_QA)
                    prefetch_slab(S_QB)
                elif u + 1 < NU:
                    prefetch_slab(S_WK)
                h2slots = {}

                def h2_load(blk_):
                    hs_ = st["h2"]
                    st["h2"] = (hs_ + 1) % NH2
                    r0_ = t0 + blk_ * 128
                    dma(H2[:, hs_, :], xq[u, r0_:r0_ + 128, :], [], [bH2[hs_], bH2h[hs_][0], bH2h[hs_][1]])
                    h2slots[blk_] = hs_
                for blk in range(NH2 - 1):
                    h2_load(blk)
                for blk in range(4):
                    if blk + NH2 - 1 < 4:
                        h2_load(blk + NH2 - 1)
                    hs = h2slots[blk]
                    r0 = t0 + blk * 128
                    for hc in range(2):
                        b = next_bank()
                        for kc in range(KC):
                            mm(PS[:, b, :], QM[:, kc, blk * 128:(blk + 1) * 128], WR[:, wo[hc], kc, :], kc == 0, kc == KC - 1,
                               [bQM[kc], bWR[wo[hc]]], [bPS[b]])
                        P.op("dve", lambda e, hs=hs, hc=hc, b=b: e.scalar_tensor_tensor(
                            out=H2[:, hs, hc * 512:(hc + 1) * 512], in0=H2[:, hs, hc * 512:(hc + 1) * 512], scalar=ALPHA,
                            in1=PS[:, b, :], op0=ALU.mult, op1=ALU.add), [bH2[hs], bPS[b]], [bH2[hs]])
                        P.op("dve", lambda e, hs=hs, hc=hc: e.bn_stats(out=ST[:, hs, hc * 6:(hc + 1) * 6],
                                                                      in_=H2[:, hs, hc * 512:(hc + 1) * 512]),
                             [bH2[hs]], [bST[hs]])
                    P.op("dve", lambda e, hs=hs: e.bn_aggr(out=MV[:, hs, 0:2], in_=ST[:, hs, :]), [bST[hs]], [bMV[hs]])
                    P.op("act", lambda e, hs=hs: e.activation(out=MV[:, hs, 2:3], in_=MV[:, hs, 1:2], func=AF.Ln,
                                                              bias=CS[:, 133:134], scale=1.0), [bMV[hs], bCS], [bMV[hs]])
                    P.op("act", lambda e, hs=hs: e.activation(out=MV[:, hs, 3:4], in_=MV[:, hs, 2:3], func=AF.Exp, scale=-0.5),
                         [bMV[hs]], [bMV[hs]])
                    P.op("dve", lambda e, hs=hs: e.tensor_scalar(
                        out=H2[:, hs, :], in0=H2[:, hs, :], scalar1=MV[:, hs, 0:1], scalar2=MV[:, hs, 3:4],
                        op0=ALU.subtract, op1=ALU.mult), [bH2[hs], bMV[hs]], [bH2[hs]])
                    for eng_, c0_, c1_ in (("pool", 0, 512), ("dve", 512, 1024)):
                        P.op(eng_, lambda e, hs=hs, c0_=c0_, c1_=c1_: e.tensor_tensor(
                            out=H2[:, hs, c0_:c1_], in0=H2[:, hs, c0_:c1_], in1=LN[:, 0, c0_:c1_], op=ALU.mult),
                            [bH2[hs], bLN], [bH2h[hs][c0_ // 512]])
                        P.op(eng_, lambda e, hs=hs, c0_=c0_, c1_=c1_: e.tensor_tensor(
                            out=H2[:, hs, c0_:c1_], in0=H2[:, hs, c0_:c1_], in1=LN[:, 1, c0_:c1_], op=ALU.add),
                            [bH2h[hs][c0_ // 512], bLN], [bH2h[hs][c0_ // 512]])
                    dma(yout[u, r0:r0 + 128, :], H2[:, hs, :], [bH2[hs], bH2h[hs][0], bH2h[hs][1]], [])

        P.finalize(nc, es)
        block = es.enter_context(nc.Block())

        @block.sync
        def _(e):
            Prog.run(e, P.thunks["sp"])
            for k, v in P.final_dma.items():
                e.wait_ge(P.dsems[k], v)

        @block.tensor
        def _(e):
            Prog.run(e, P.thunks["pe"])

        @block.scalar
        def _(e):
            Prog.run(e, P.thunks["act"])

        @block.vector
        def _(e):
            Prog.run(e, P.thunks["dve"])

        @block.gpsimd
        def _(e):
            Prog.run(e, P.thunks["pool"])

    return nc


PERM_B = np.array(list(range(0, 16)) + list(range(32, 48)) + list(range(16, 32)) + list(range(48, 64)))
OFF = dict(qa=0, ka=512, va=640, za=768, qb=1280, kb=1792, vb=1920, zb=2048, ga=2560, gb=3584)


def _head_cols(off, c, perm):
    return np.concatenate([off + c * 64 + perm, off + (4 + c) * 64 + perm])


def make_slabs(w_in, w_ba, w_bb, w_out):
    nat = np.arange(64)
    cols = {}
    cols[S_QA] = np.concatenate([_head_cols(OFF["qa"], c, nat) for c in range(4)])
    cols[S_QB] = np.concatenate([_head_cols(OFF["qb"], c, PERM_B) for c in range(4)])
    cols[S_ZA] = np.concatenate([_head_cols(OFF["za"], c, nat) for c in range(4)])
    cols[S_ZB] = np.concatenate([_head_cols(OFF["zb"], c, nat) for c in range(4)])
    cols[S_GA0] = OFF["ga"] + np.arange(0, 512)
    cols[S_GA1] = OFF["ga"] + np.arange(512, 1024)
    cols[S_GB0] = OFF["gb"] + np.arange(0, 512)
    cols[S_GB1] = OFF["gb"] + np.arange(512, 1024)
    cols[S_WK] = np.concatenate([OFF["ka"] + np.arange(128),
                                 OFF["kb"] + PERM_B, OFF["kb"] + 64 + PERM_B,
                                 OFF["va"] + np.arange(128), OFF["vb"] + np.arange(128)])
    slabs = np.zeros((NSLAB, 128, KC, 512), np.float32)
    for s, cc in cols.items():
        slabs[s] = w_in[:, cc].reshape(KC, 128, 512).transpose(1, 0, 2)
    rowperm = np.concatenate([np.concatenate([c * 64 + nat, (4 + c) * 64 + nat]) for c in range(4)])
    for s, w in ((S_WBA, w_ba), (S_WBB, w_bb)):
        t = w[rowperm, :].reshape(4, 128, 1024).transpose(1, 0, 2)
        slabs[s] = t.reshape(128, 4, 2, 512).reshape(128, 8, 512)
    for hc, s in ((0, S_WO0), (1, S_WO1)):
        slabs[s] = w_out[:, hc * 512:(hc + 1) * 512].reshape(KC, 128, 512).transpose(1, 0, 2)
    return slabs


def make_tables(NT, S, hf):
    own = np.arange(hf * NT, (hf + 1) * NT)
    oth = np.arange((1 - hf) * NT, (2 - hf) * NT)
    real = np.concatenate([own, oth])
    metai = np.arange(N_META)
    posA = np.concatenate([N_META + real, metai]).astype(np.float32)
    rowB = np.concatenate([real // GRID_W, metai - N_META]).astype(np.float32)
    colB = np.concatenate([real % GRID_W, metai - N_META]).astype(np.float32)
    LK = 2 * NT + N_META
    inv32 = (ROPE_THETA ** (-np.arange(0, 64, 2, dtype=np.float32) / 64)).astype(np.float32)
    inv16 = (ROPE_THETA ** (-np.arange(0, 32, 2, dtype=np.float32) / 32)).astype(np.float32)
    tab = np.zeros((4, 128, LK), np.float32)
    for p in range(128):
        d = p % 64
        angA = posA * inv32[d % 32]
        tab[0, p] = np.cos(angA)
        tab[1, p] = np.sin(angA) * (-1.0 if d < 32 else 1.0)
        od = PERM_B[d]
        pos = rowB if od < 32 else colB
        angB = pos * inv16[od % 16]
        tab[2, p] = np.cos(angB)
        tab[3, p] = np.sin(angB) * (-1.0 if (od % 32) < 16 else 1.0)
    return tab


def make_masks(hf):
    m = np.zeros((8, 128, 512), np.float32)
    j = np.arange(128)[:, None]
    i = np.arange(128)[None, :]
    for r in range(-1, 5):
        for qb in range(4):
            rel = (r - qb) * 128 + j - i
            m[r + 1, :, qb * 128:(qb + 1) * 128] = np.where(np.abs(rel) <= 128, 0.0, NEG)
    m[6] = m[0] if hf == 1 else NEG
    m[7] = m[5] if hf == 0 else NEG
    return m.astype(ml_dtypes.bfloat16)


def make_consts(q_norm_b, k_norm_b):
    cs = np.zeros((128, 134), np.float32)
    cs[:, 132] = RMS_EPS
    cs[:, 133] = LN_EPS
    p = np.arange(128)
    cs[:, 0:128] = (p[:, None] // 64 == p[None, :] // 64).astype(np.float32)
    d = p % 64
    sw = np.where(d < 32, d + 32, d - 32)
    cs[:, 128] = q_norm_b[PERM_B[d]]
    cs[:, 129] = q_norm_b[PERM_B[sw]]
    cs[:, 130] = k_norm_b[PERM_B[d]]
    cs[:, 131] = k_norm_b[PERM_B[sw]]
    return cs


def unit_inputs(x_seq, meta_tokens, hf, NT):
    own = x_seq[hf * NT:(hf + 1) * NT]
    oth = x_seq[(1 - hf) * NT:(2 - hf) * NT]
    xT = np.ascontiguousarray(np.concatenate([own, oth, meta_tokens], axis=0).T)
    return xT, np.ascontiguousarray(own)


_NC_CACHE = {}


def kernel(x_prompt, x_sample, meta_tokens, w_in, attn_a_sink, q_norm_b, k_norm_b,
           w_branch_a, w_branch_b, w_out, ln_gain, ln_bias):
    f = lambda a: np.asarray(a, dtype=np.float32)
    x_prompt, x_sample, meta_tokens = f(x_prompt), f(x_sample), f(meta_tokens)
    S = x_prompt.shape[1]
    NT = S // 2
    seqs = [x_prompt[b] for b in range(x_prompt.shape[0])] + [x_sample[b] for b in range(x_sample.shape[0])]
    n_units = 2 * len(seqs)
    NU = n_units // N_CORES
    assert NU * N_CORES == n_units
    slabs = make_slabs(f(w_in)[0], f(w_branch_a)[0], f(w_branch_b)[0], f(w_out)[0])
    cs = make_consts(f(q_norm_b)[0], f(k_norm_b)[0])
    ident = np.eye(128, dtype=np.float32).astype(ml_dtypes.bfloat16)
    lngb = np.stack([np.broadcast_to(f(ln_gain)[0][None, :], (128, D_MODEL)),
                     np.broadcast_to(f(ln_bias)[0][None, :], (128, D_MODEL))]).astype(np.float32)
    sink = f(attn_a_sink)[0][None, :]
    tabs_hf = [make_tables(NT, S, 0), make_tables(NT, S, 1)]
    masks_hf = [make_masks(0), make_masks(1)]
    in_maps = []
    for c in range(N_CORES):
        xTs, xqs, tbs, mks = [], [], [], []
        for j in range(NU):
            uid = c * NU + j
            s, hf = uid // 2, uid % 2
            xT_, xq_ = unit_inputs(seqs[s], meta_tokens, hf, NT)
            xTs.append(xT_)
            xqs.append(xq_)
            tbs.append(tabs_hf[hf])
            mks.append(masks_hf[hf])
        in_maps.append(dict(xT=np.stack(xTs), xq=np.stack(xqs), tabs=np.stack(tbs), masks=np.stack(mks),
                            wsrc=slabs, cst=cs, ident=ident, lngb=lngb, sink=sink))
    key = (NT, NU)
    if key not in _NC_CACHE:
        _NC_CACHE[key] = build_program(NT, NU)
    nc = _NC_CACHE[key]
    res = run_bass_kernel_spmd(nc, in_maps, core_ids=list(range(N_CORES)))
    outs = [np.zeros((S, D_MODEL), np.float32) for _ in seqs]
    for c in range(N_CORES):
        yc = np.asarray(res.results[c]["y"])
        for j in range(NU):
            uid = c * NU + j
            s, hf = uid // 2, uid % 2
            outs[s][hf * NT:(hf + 1) * NT] = yc[j]
    nb = x_prompt.shape[0]
    y_prompt = np.stack(outs[:nb]).astype(np.float32)
    y_sample = np.stack(outs[nb:]).astype(np.float32)
    return (y_prompt, y_sample)
```
